# Optimizing a Trainium2 kernel written in Bass

```python
import math
import jax, jax.numpy as jnp
from jax import lax
import numpy as np

D_MODEL = 2048
BATCH = 16
SEQ = 256
DEPTH = 1
DEC_BATCH = 8
DEC_SEQ = 2048
PAST_LEN = 256

GRID_W = 64
MLA_HEADS = 8
QK_NOPE = 128
QK_ROPE = 64
V_HEAD = 128
Q_LORA = 512
KV_LORA = 256
MLA_WIDTH = MLA_HEADS * V_HEAD
ROPE_AXIS_FREQS = QK_ROPE // 4
ROPE_THETA = 10000.0
Q_BLOCK = 128
RWKV_HEADS = 16
RWKV_HEAD = 64
RWKV_WIDTH = RWKV_HEADS * RWKV_HEAD
DECAY_LORA = 64
ICLR_LORA = 64
GATE_LORA = 128
MIX_WIDTH = MLA_WIDTH + RWKV_WIDTH
OFF_KV = Q_LORA
OFF_KR = OFF_KV + KV_LORA
OFF_RW = OFF_KR + QK_ROPE
RW_COLS = 3 * RWKV_WIDTH + DECAY_LORA + ICLR_LORA + GATE_LORA
IN_COLS = OFF_RW + RW_COLS
D_FF = ((8 * D_MODEL + 3 * 256 - 1) // (3 * 256)) * 256
LN_EPS = 1e-5
RMS_EPS = 1e-6
GN_EPS = 64e-5
ALPHA = (2.0 * DEPTH) ** 0.25
BETA = (8.0 * DEPTH) ** -0.25

kernel_name = "hymba_mla_rwkv7_flow_step"

F32 = jnp.float32


def _layer_norm(x, g, b):
    xf = x.astype(F32)
    mu = jnp.mean(xf, -1, keepdims=True)
    var = jnp.mean(jnp.square(xf - mu), -1, keepdims=True)
    return ((xf - mu) * lax.rsqrt(var + LN_EPS) * g + b).astype(x.dtype)


def _rms_norm(x, g):
    xf = x.astype(F32)
    return (xf * lax.rsqrt(jnp.mean(xf * xf, -1, keepdims=True) + RMS_EPS) * g).astype(x.dtype)


def _adaln(cond, w_mod, b_mod):
    m = jax.nn.silu(cond) @ w_mod + b_mod
    return jnp.split(m[:, None, :], 6, axis=-1)


def _post_norm(x, gate, f, g, b):
    return _layer_norm(ALPHA * x + gate * f, g, b)


def _axial_rope_angles(n_tokens):
    rows = n_tokens // GRID_W
    row = jnp.repeat(jnp.arange(rows), GRID_W).astype(F32)
    col = jnp.tile(jnp.arange(GRID_W), rows).astype(F32)
    freqs = ROPE_THETA ** (-jnp.arange(ROPE_AXIS_FREQS, dtype=F32) / ROPE_AXIS_FREQS)
    ang = jnp.concatenate([row[:, None] * freqs, col[:, None] * freqs], -1)
    return jnp.cos(ang), jnp.sin(ang)


def _apply_rope(x, cos, sin):
    half = QK_ROPE // 2
    x1, x2 = x[..., :half].astype(F32), x[..., half:].astype(F32)
    return jnp.concatenate([x1 * cos - x2 * sin, x2 * cos + x1 * sin], -1).astype(x.dtype)


def _mla_queries(q_down, p):
    b, n, _ = q_down.shape
    q = (_rms_norm(q_down, p["q_norm_g"]) @ p["w_uq"]).reshape(b, n, MLA_HEADS, QK_NOPE + QK_ROPE)
    return q[..., :QK_NOPE], q[..., QK_NOPE:]


def _mla_kv(ckv, p):
    b, n, _ = ckv.shape
    k_nope = (ckv @ p["w_uk"]).reshape(b, n, MLA_HEADS, QK_NOPE)
    v = (ckv @ p["w_uv"]).reshape(b, n, MLA_HEADS, V_HEAD)
    return k_nope, v


def _attend(q_nope, q_rope, k_nope, k_rope, v):
    scale = 1.0 / math.sqrt(QK_NOPE + QK_ROPE)
    s = jnp.einsum("bqhd,bkhd->bhqk", q_nope, k_nope) + jnp.einsum("bqhd,bkd->bhqk", q_rope, k_rope)
    pr = jax.nn.softmax(s.astype(F32) * scale, axis=-1).astype(v.dtype)
    return jnp.einsum("bhqk,bkhd->bqhd", pr, v)


def _attend_blocked(q_nope, q_rope, k_nope, k_rope, v):
    b, n, h, _ = q_nope.shape
    nb = n // Q_BLOCK
    to_blocks = lambda t: jnp.moveaxis(t.reshape(b, nb, Q_BLOCK, *t.shape[2:]), 1, 0)
    out = lax.map(lambda qs: _attend(qs[0], qs[1], k_nope, k_rope, v), (to_blocks(q_nope), to_blocks(q_rope)))
    return jnp.moveaxis(out, 0, 1).reshape(b, n, h, V_HEAD)


def _centred_shift(u, mu):
    prev = jnp.pad(u[:, :-1], ((0, 0), (1, 0), (0, 0)))
    nxt = jnp.pad(u[:, 1:], ((0, 0), (0, 1), (0, 0)))
    return u + mu * (0.5 * (prev + nxt) - u)


def _wkv_scan(s0, r, decay, k, v, kk, a, reverse):
    to_t = lambda t: jnp.moveaxis(t.astype(F32), 1, 0)

    def step(S, inp):
        r_t, w_t, k_t, v_t, kk_t, a_t = inp
        sa = jnp.einsum("bhvk,bhk->bhv", S, -kk_t)
        S = S * w_t[:, :, None, :] + sa[..., None] * (kk_t * a_t)[:, :, None, :] + v_t[..., None] * k_t[:, :, None, :]
        return S, jnp.einsum("bhvk,bhk->bhv", S, r_t)

    s_fin, ys = lax.scan(step, s0.astype(F32), (to_t(r), to_t(decay), to_t(k), to_t(v), to_t(kk), to_t(a)), reverse=reverse)
    return s_fin, jnp.moveaxis(ys, 0, 1)


def _rwkv_mixer(u, s0_fwd, s0_bwd, p):
    b, n, _ = u.shape
    u = _centred_shift(u, p["tok_shift_mu"])
    W = RWKV_WIDTH
    r, k, v = u[..., :W], u[..., W:2 * W], u[..., 2 * W:3 * W]
    wd = u[..., 3 * W:3 * W + DECAY_LORA]
    ad = u[..., 3 * W + DECAY_LORA:3 * W + DECAY_LORA + ICLR_LORA]
    gd = u[..., 3 * W + DECAY_LORA + ICLR_LORA:]
    heads = lambda t: t.reshape(b, n, RWKV_HEADS, RWKV_HEAD)
    kk = heads(k * p["k_k"]).astype(F32)
    kk = kk * lax.rsqrt(jnp.maximum(jnp.sum(kk * kk, -1, keepdims=True), 1e-24))
    g = jax.nn.sigmoid(gd) @ p["g_up"]
    r_h, v_h = heads(r), heads(v)
    r_k = p["r_k"].reshape(RWKV_HEADS, RWKV_HEAD)
    ys, bonuses, states = [], [], []
    for w0, w_up, a0, a_up, s0, rev in ((p["w0_fwd"], p["w_up_fwd"], p["a0_fwd"], p["a_up_fwd"], s0_fwd, False),
                                       (p["w0_bwd"], p["w_up_bwd"], p["a0_bwd"], p["a_up_bwd"], s0_bwd, True)):
        logw = -jax.nn.softplus(-(w0 + jnp.tanh(wd) @ w_up).astype(F32)) - 0.5
        decay = jnp.exp(-jnp.exp(logw))
        a = jax.nn.sigmoid(a0 + ad @ a_up)
        k_dir = heads(k * (1 + (a - 1) * p["k_a"]))
        s_fin, y_dir = _wkv_scan(s0, r_h, heads(decay), k_dir, v_h, kk, heads(a), rev)
        ys.append(y_dir)
        states.append(s_fin)
        bonuses.append(jnp.sum((r_h * k_dir * r_k).astype(F32), -1, keepdims=True) * v_h.astype(F32))
    y = ys[0] + ys[1]
    mu = jnp.mean(y, -1, keepdims=True)
    var = jnp.mean(jnp.square(y - mu), -1, keepdims=True)
    y = ((y - mu) * lax.rsqrt(var + GN_EPS)).reshape(b, n, W) * p["gn_g"] + p["gn_b"]
    y = (y + (bonuses[0] + bonuses[1]).reshape(b, n, W)) * g
    return y.astype(u.dtype), states[0], states[1]


def _merge(att, rw, p):
    b, n = rw.shape[:2]
    return jnp.concatenate([att.reshape(b, n, MLA_WIDTH), rw], -1) @ p["w_out"]


def _swiglu(h, p):
    return (jax.nn.silu(h @ p["w_ffn_gate"]) * (h @ p["w_ffn_up"])) @ p["w_ffn_down"]


def _context_layer(x, c_ctx, p):
    b, n, _ = x.shape
    sh1, sc1, g1, sh2, sc2, g2 = _adaln(c_ctx[None, :], p["w_mod"], p["b_mod"])
    proj = (x * (1 + sc1) + sh1) @ p["w_in"]
    q_nope, q_rope = _mla_queries(proj[..., :OFF_KV], p)
    ckv = _rms_norm(proj[..., OFF_KV:OFF_KR], p["kv_norm_g"])
    k_rope = proj[..., OFF_KR:OFF_RW]
    k_nope, v = _mla_kv(ckv, p)
    att = _attend(q_nope, q_rope, k_nope, k_rope, v)
    zeros = jnp.zeros((b, RWKV_HEADS, RWKV_HEAD, RWKV_HEAD), F32)
    rw, s_fwd, s_bwd = _rwkv_mixer(proj[..., OFF_RW:], zeros, zeros, p)
    x = _post_norm(x, g1, _merge(att, rw, p), p["ln1_g"], p["ln1_b"])
    x = _post_norm(x, g2, _swiglu(x * (1 + sc2) + sh2, p), p["ln2_g"], p["ln2_b"])
    return x, ckv, k_rope, s_fwd, s_bwd


def _latent_layer(x, c, ckv_ctx, krope_ctx, s_fwd, s_bwd, p):
    b, n, _ = x.shape
    sh1, sc1, g1, sh2, sc2, g2 = _adaln(c, p["w_mod"], p["b_mod"])
    proj = (x * (1 + sc1) + sh1) @ p["w_in"]
    cos, sin = _axial_rope_angles(n)
    q_nope, q_rope = _mla_queries(proj[..., :OFF_KV], p)
    q_rope = _apply_rope(q_rope, cos[:, None, :], sin[:, None, :])
    ckv = _rms_norm(proj[..., OFF_KV:OFF_KR], p["kv_norm_g"])
    k_rope = _apply_rope(proj[..., OFF_KR:OFF_RW], cos, sin)
    k_nope, v = _mla_kv(ckv, p)
    k_nope_c, v_c = _mla_kv(ckv_ctx, p)
    att = _attend_blocked(q_nope, q_rope,
                          jnp.concatenate([k_nope, k_nope_c], 1),
                          jnp.concatenate([k_rope, krope_ctx.astype(k_rope.dtype)], 1),
                          jnp.concatenate([v, v_c], 1))
    rw, _, _ = _rwkv_mixer(proj[..., OFF_RW:], s_fwd, s_bwd, p)
    x = _post_norm(x, g1, _merge(att, rw, p), p["ln1_g"], p["ln1_b"])
    x = _post_norm(x, g2, _swiglu(x * (1 + sc2) + sh2, p), p["ln2_g"], p["ln2_b"])
    return x


def setup_inputs(seed: int = 0) -> dict:
    key = jax.random.key(seed)
    ks = iter(jax.random.split(key, 48))
    nrm = lambda shape, scale: jax.random.normal(next(ks), shape, F32) * scale
    uni = lambda shape, lo, hi: jax.random.uniform(next(ks), shape, F32, lo, hi)
    L = DEPTH
    return {
        "x_prompt": nrm((BATCH, SEQ, D_MODEL), 1.0),
        "x_sample": nrm((DEC_BATCH, DEC_SEQ, D_MODEL), 1.0),
        "cache_ckv": nrm((DEC_BATCH, L, PAST_LEN, KV_LORA), 1.0),
        "cache_krope": nrm((DEC_BATCH, L, PAST_LEN, QK_ROPE), 1.0),
        "state_wkv_fwd": nrm((DEC_BATCH, L, RWKV_HEADS, RWKV_HEAD, RWKV_HEAD), 0.5),
        "state_wkv_bwd": nrm((DEC_BATCH, L, RWKV_HEADS, RWKV_HEAD, RWKV_HEAD), 0.5),
        "c": nrm((DEC_BATCH, D_MODEL), 1.0),
        "c_ctx": nrm((D_MODEL,), 1.0),
        "w_mod": nrm((L, D_MODEL, 6 * D_MODEL), 0.5 * D_MODEL ** -0.5),
        "b_mod": nrm((L, 6 * D_MODEL), 0.02),
        "w_in": nrm((L, D_MODEL, IN_COLS), D_MODEL ** -0.5),
        "q_norm_g": 1.0 + nrm((L, Q_LORA), 0.1),
        "kv_norm_g": 1.0 + nrm((L, KV_LORA), 0.1),
        "w_uq": nrm((L, Q_LORA, MLA_HEADS * (QK_NOPE + QK_ROPE)), Q_LORA ** -0.5),
        "w_uk": nrm((L, KV_LORA, MLA_HEADS * QK_NOPE), KV_LORA ** -0.5),
        "w_uv": nrm((L, KV_LORA, MLA_HEADS * V_HEAD), KV_LORA ** -0.5),
        "tok_shift_mu": uni((L, RW_COLS), 0.0, 1.0),
        "w0_fwd": uni((L, RWKV_WIDTH), -4.0, 1.0),
        "w_up_fwd": nrm((L, DECAY_LORA, RWKV_WIDTH), 0.5 * DECAY_LORA ** -0.5),
        "a0_fwd": nrm((L, RWKV_WIDTH), 0.1),
        "a_up_fwd": nrm((L, ICLR_LORA, RWKV_WIDTH), 0.5 * ICLR_LORA ** -0.5),
        "w0_bwd": uni((L, RWKV_WIDTH), -4.0, 1.0),
        "w_up_bwd": nrm((L, DECAY_LORA, RWKV_WIDTH), 0.5 * DECAY_LORA ** -0.5),
        "a0_bwd": nrm((L, RWKV_WIDTH), 0.1),
        "a_up_bwd": nrm((L, ICLR_LORA, RWKV_WIDTH), 0.5 * ICLR_LORA ** -0.5),
        "g_up": nrm((L, GATE_LORA, RWKV_WIDTH), GATE_LORA ** -0.5),
        "k_k": 0.85 + nrm((L, RWKV_WIDTH), 0.05),
        "k_a": 1.0 + nrm((L, RWKV_WIDTH), 0.05),
        "r_k": nrm((L, RWKV_WIDTH), 0.1),
        "gn_g": 1.0 + nrm((L, RWKV_WIDTH), 0.1),
        "gn_b": nrm((L, RWKV_WIDTH), 0.02),
        "w_out": nrm((L, MIX_WIDTH, D_MODEL), BETA * MIX_WIDTH ** -0.5),
        "ln1_g": 1.0 + nrm((L, D_MODEL), 0.1),
        "ln1_b": nrm((L, D_MODEL), 0.02),
        "w_ffn_gate": nrm((L, D_MODEL, D_FF), D_MODEL ** -0.5),
        "w_ffn_up": nrm((L, D_MODEL, D_FF), D_MODEL ** -0.5),
        "w_ffn_down": nrm((L, D_FF, D_MODEL), BETA * D_FF ** -0.5),
        "ln2_g": 1.0 + nrm((L, D_MODEL), 0.1),
        "ln2_b": nrm((L, D_MODEL), 0.02),
    }


def reference(x_prompt, x_sample, cache_ckv, cache_krope, state_wkv_fwd, state_wkv_bwd, c, c_ctx,
              w_mod, b_mod, w_in, q_norm_g, kv_norm_g, w_uq, w_uk, w_uv, tok_shift_mu,
              w0_fwd, w_up_fwd, a0_fwd, a_up_fwd, w0_bwd, w_up_bwd, a0_bwd, a_up_bwd,
              g_up, k_k, k_a, r_k, gn_g, gn_b, w_out, ln1_g, ln1_b,
              w_ffn_gate, w_ffn_up, w_ffn_down, ln2_g, ln2_b):
    weights = dict(w_mod=w_mod, b_mod=b_mod, w_in=w_in, q_norm_g=q_norm_g, kv_norm_g=kv_norm_g,
                   w_uq=w_uq, w_uk=w_uk, w_uv=w_uv, tok_shift_mu=tok_shift_mu,
                   w0_fwd=w0_fwd, w_up_fwd=w_up_fwd, a0_fwd=a0_fwd, a_up_fwd=a_up_fwd,
                   w0_bwd=w0_bwd, w_up_bwd=w_up_bwd, a0_bwd=a0_bwd, a_up_bwd=a_up_bwd,
                   g_up=g_up, k_k=k_k, k_a=k_a, r_k=r_k, gn_g=gn_g, gn_b=gn_b, w_out=w_out,
                   ln1_g=ln1_g, ln1_b=ln1_b, w_ffn_gate=w_ffn_gate, w_ffn_up=w_ffn_up,
                   w_ffn_down=w_ffn_down, ln2_g=ln2_g, ln2_b=ln2_b)
    y_prompt, y_sample = x_prompt, x_sample
    ckv_list, krope_list, sf_list, sb_list = [], [], [], []
    for l in range(DEPTH):
        p = {name: arr[l] for name, arr in weights.items()}
        y_prompt, ckv, krope, s_f, s_b = _context_layer(y_prompt, c_ctx, p)
        ckv_list.append(ckv)
        krope_list.append(krope)
        sf_list.append(s_f)
        sb_list.append(s_b)
        y_sample = _latent_layer(y_sample, c, cache_ckv[:, l], cache_krope[:, l],
                                 state_wkv_fwd[:, l], state_wkv_bwd[:, l], p)
    new_ckv = jnp.stack(ckv_list, 1)
    new_krope = jnp.stack(krope_list, 1)
    new_state_fwd = jnp.stack(sf_list, 1)
    new_state_bwd = jnp.stack(sb_list, 1)
    return (y_prompt, y_sample, new_ckv, new_krope, new_state_fwd, new_state_bwd)
```

```python
import math
from contextlib import ExitStack
import numpy as np
import concourse.bass as bass
import concourse.mybir as mybir
from concourse.bass_utils import run_bass_kernel_spmd

F32 = mybir.dt.float32
BF16 = mybir.dt.bfloat16
F32R = mybir.dt.float32r
AF = mybir.ActivationFunctionType
ALU = mybir.AluOpType

D = 2048
NS = 2048
NP_ = 256
G = NS + 2 * NP_
IN_COLS = 4160
PROJ_ROWS = 4224
OFF_KV, OFF_KR, OFF_RW = 512, 768, 832
DFF = 5632
ALPHA = 2.0 ** 0.25
LN_EPS = 1e-5
RMS_EPS = 1e-6
GN_EPS = 64e-5
SCALE = 1.0 / math.sqrt(192.0)
NEG_EXP_HALF = -math.exp(-0.5)
SEM_LIMIT = 30000
STRICT_SAME = "waw"
DSTOP = 99
POOL_ENG = "dve"
TDT = F32
TSW = 7
PREC = {"lora": BF16, "scan": BF16, "T": F32R, "Tt": BF16}
C = 128


class Res:
    __slots__ = ("lw", "rd", "excl")

    def __init__(self, excl=False):
        self.lw = None
        self.rd = []
        self.excl = excl


class Tl:
    __slots__ = ("ap", "r")

    def __init__(self, ap, r=None):
        self.ap = ap
        self.r = r if r is not None else Res()


class Prog:
    STREAMS = ("pe", "act", "dve", "pool", "sp")

    def __init__(self, nc, stack):
        self.nc = nc
        self.stack = stack
        self.q = {s: [] for s in self.STREAMS}
        self.chains = {}
        self.waited = {s: {} for s in self.STREAMS}
        self.ev_i = 0
        self.dma_slots = {}

    def _chain(self, key):
        c = self.chains.get(key)
        if c is None:
            h = self.stack.enter_context(self.nc.semaphore(f"s_{key}_0"))
            c = dict(sems=[h], cur=0, cnt=0)
            self.chains[key] = c
        return c

    def _bump(self, key, step):
        c = self._chain(key)
        if c["cnt"] + step > SEM_LIMIT:
            h = self.stack.enter_context(self.nc.semaphore(f"s_{key}_{len(c['sems'])}"))
            c["sems"].append(h)
            c["cur"] += 1
            c["cnt"] = 0
        c["cnt"] += step
        return (key, c["cur"]), c["cnt"]

    def _peek(self, key, step):
        c = self._chain(key)
        if c["cnt"] + step > SEM_LIMIT:
            return (key, c["cur"] + 1), step
        return (key, c["cur"]), c["cnt"] + step

    def op(self, stream, fn, reads=(), writes=(), inc=True, semkey=None, step=1):
        if semkey is None:
            semkey = stream
        need = {}

        def add(dep, mode):
            if dep is None:
                return
            sk, val, st = dep
            if st == stream and sk[0] == stream:
                if mode == "rr":
                    return
                if mode != "raw" and (stream == "pe" or not STRICT_SAME):
                    return
                if mode == "war" and STRICT_SAME == "waw":
                    return
            if need.get(sk, 0) < val:
                need[sk] = val

        for r in reads:
            add(r.lw, "raw")
            if r.excl:
                for d in r.rd:
                    add(d, "rr")
        for w in writes:
            add(w.lw, "waw")
            for d in w.rd:
                add(d, "war")
        wt = self.waited[stream]
        waits = []
        for sk, val in need.items():
            newer = sk[0] in self.STREAMS and any(k2[0] == sk[0] and k2[1] > sk[1] for k2 in wt)
            if newer or wt.get(sk, 0) >= val:
                continue
            wt[sk] = val
            waits.append((sk, val))
        if inc:
            sk, val = self._bump(semkey, step)
        else:
            sk, val = self._peek(semkey, step)
        me = (sk, val, stream)
        for r in reads:
            r.rd.append(me)
        for w in writes:
            w.lw = me
            w.rd = []
        self.q[stream].append((waits, fn, (sk if inc else None), step))

    def sem_handle(self, sk):
        return self.chains[sk[0]]["sems"][sk[1]]

    def barrier(self, streams=None):
        if streams is None:
            self.dma_slots = {}
        for s in (streams or self.STREAMS):
            waits = []
            wt = self.waited[s]
            for key, c in self.chains.items():
                if c["cnt"] == 0:
                    continue
                sk = (key, c["cur"])
                if wt.get(sk, 0) >= c["cnt"]:
                    continue
                wt[sk] = c["cnt"]
                waits.append((sk, c["cnt"]))
            if waits:
                self.q[s].append((waits, None, None, 0))

    def emit(self):
        nc = self.nc
        prog = self
        with nc.Block() as block:
            def run(stream):
                def body(eng):
                    for waits, fn, sk, step in prog.q[stream]:
                        for wsk, val in waits:
                            eng.wait_ge(prog.sem_handle(wsk), val)
                        if fn is None:
                            continue
                        ins = fn(eng)
                        if sk is not None:
                            ins.then_inc(prog.sem_handle(sk), step)
                return body
            block.tensor(run("pe"))
            block.scalar(run("act"))
            block.vector(run("dve"))
            block.gpsimd(run("pool"))
            block.sync(run("sp"))

    def ev(self):
        self.ev_i += 1
        return "act" if self.ev_i % 2 else "dve"

    def dma(self, out, in_, reads=(), writes=(), q="sp", key="ld"):
        res = writes[0] if len(writes) else reads[0]
        fam = "w" if q == "pool" else "h"
        slot = self.dma_slots.get((fam, id(res)))
        if slot is None:
            slot = sum(1 for k_ in self.dma_slots if k_[0] == fam)
            self.dma_slots[(fam, id(res))] = slot
        self.op(q, lambda e: e.dma_start(out=out, in_=in_), reads, writes, semkey=f"dma{fam}{slot}", step=16)

    def mm(self, out, lhsT, rhs, start, stop, reads, writes, inc=None):
        if inc is None:
            inc = stop
        self.op("pe", lambda e: e.matmul(out, lhsT, rhs, start=start, stop=stop), reads, writes, inc=inc)

    def tr(self, out, in_, ident, reads, writes, inc=True):
        self.op("pe", lambda e: e.transpose(out, in_, ident), reads, writes, inc=inc)

    def copy(self, eng, out, in_, reads, writes):
        if eng == "act":
            self.op("act", lambda e: e.copy(out, in_), reads, writes)
        else:
            self.op(eng, lambda e: e.tensor_copy(out, in_), reads, writes)

    def act(self, out, in_, func, reads, writes, scale=1.0, bias=0.0, accum=None):
        kw = {}
        if not (isinstance(bias, float) and bias == 0.0):
            kw["bias"] = bias
        if not (isinstance(scale, float) and scale == 1.0):
            kw["scale"] = scale
        if accum is not None:
            kw["accum_out"] = accum
        self.op("act", lambda e: e.activation(out=out, in_=in_, func=func, **kw), reads, writes)

    def affine(self, eng, out, in_, scale, bias, reads, writes):
        nobias = isinstance(bias, float) and bias == 0.0
        if eng == "act":
            self.act(out, in_, AF.Identity, reads, writes, scale=scale, bias=bias)
        elif nobias:
            self.op("dve", lambda e: e.tensor_scalar(out, in_, scale, None, op0=ALU.mult), reads, writes)
        else:
            self.op("dve", lambda e: e.tensor_scalar(out, in_, scale, bias, op0=ALU.mult, op1=ALU.add), reads, writes)

    def call(self, stream, name, args, reads, writes, **kw):
        self.op(stream, lambda e: getattr(e, name)(*args, **kw), reads, writes)

    def tt(self, out, a, b, op, reads, writes, eng="dve"):
        self.op(eng, lambda e: e.tensor_tensor(out, a, b, op), reads, writes)

    def ts(self, out, a, s1, s2, op0, op1, reads, writes, eng="dve"):
        if s2 is None:
            self.op(eng, lambda e: e.tensor_scalar(out, a, s1, None, op0=op0), reads, writes)
        else:
            self.op(eng, lambda e: e.tensor_scalar(out, a, s1, s2, op0=op0, op1=op1), reads, writes)

    def stt(self, out, in0, scalar, in1, op0, op1, reads, writes, eng="dve"):
        self.op(eng, lambda e: e.scalar_tensor_tensor(out, in0, scalar, in1, op0, op1), reads, writes)


class SBAlloc:
    def __init__(self, big, nwords):
        self.big = big
        self.n = nwords
        self.off = 0

    def a(self, free_shape, dt=F32, parts=(0, 128)):
        n = int(np.prod(free_shape))
        words = n if dt in (F32, F32R) else (n + 1) // 2
        words = (words + 7) // 8 * 8
        assert self.off + words <= self.n, f"SBUF overflow {self.off}+{words}>{self.n}"
        ap = self.big[parts[0]:parts[1], self.off:self.off + words]
        self.off += words
        if dt != F32:
            ap = ap.bitcast(dt)
        ap = ap[:, 0:n]
        if len(free_shape) == 2:
            ap = ap.rearrange("p (a b) -> p a b", a=free_shape[0])
        elif len(free_shape) == 3:
            ap = ap.rearrange("p (a b c) -> p a b c", a=free_shape[0], b=free_shape[1])
        elif len(free_shape) == 4:
            ap = ap.rearrange("p (a b c d) -> p a b c d", a=free_shape[0], b=free_shape[1], c=free_shape[2])
        return ap

    def t(self, free_shape, dt=F32, parts=(0, 128)):
        return Tl(self.a(free_shape, dt, parts))

    def ring(self, n, free_shape, dt=F32):
        return [self.t(free_shape, dt) for _ in range(n)]


class Ring:
    def __init__(self, items):
        self.items = items
        self.i = -1

    def next(self):
        self.i = (self.i + 1) % len(self.items)
        return self.items[self.i]

    def take(self):
        it = self.next()
        self.items.remove(it)
        self.i -= 1
        return it

    def give(self, it):
        self.items.append(it)


SB_WORDS = 44 * 1024
SBR_WORDS = 12 * 512


def build_program(debug=False, stages="ABCDEF", dbg_seqs=None, dbg_pairs=None):
    nc = bass.Bass("TRN2", target_bir_lowering=False)
    dt_in = lambda name, shape: nc.dram_tensor(name, list(shape), F32, kind="ExternalInput").ap()
    dt_out = lambda name, shape: nc.dram_tensor(name, list(shape), F32, kind="ExternalOutput").ap()
    scr_kind = "ExternalOutput" if debug else "Internal"
    x_all = dt_in("x_all", [G, D])
    cache_ckv = dt_in("cache_ckv", [256, 256])
    cache_kr = dt_in("cache_kr", [256, 64])
    st_f = dt_in("st_f", [16, 64, 64])
    st_b = dt_in("st_b", [16, 64, 64])
    condT = dt_in("condT", [128, 16, 2])
    bmodT = dt_in("bmodT", [128, 96])
    vecs = dt_in("vecs", [128, NVEC])
    consts = dt_in("consts", [128, NCONST])
    ropeT = dt_in("ropeT", [64, 2, NS])
    w_mod = dt_in("w_mod", [D, 6 * D])
    w_in = dt_in("w_in", [D, IN_COLS])
    w_uq = dt_in("w_uq", [512, 1536])
    w_uk = dt_in("w_uk", [256, 1024])
    w_uv = dt_in("w_uv", [256, 1024])
    w_up_f = dt_in("w_up_f", [64, 1024])
    w_up_b = dt_in("w_up_b", [64, 1024])
    a_up_f = dt_in("a_up_f", [64, 1024])
    a_up_b = dt_in("a_up_b", [64, 1024])
    g_up = dt_in("g_up", [128, 1024])
    w_out = dt_in("w_out", [D, D])
    lnv = dt_in("lnv", [4, D])
    w_gate = dt_in("w_gate", [D, DFF])
    w_upf = dt_in("w_upf", [D, DFF])
    w_down = dt_in("w_down", [DFF, D])
    y_all = dt_out("y_all", [G, D])
    nckv = dt_out("nckv", [2 * NP_, 256])
    nkr = dt_out("nkr", [2 * NP_, 64])
    nsf = dt_out("nsf", [2, 16, 64, 64])
    nsb = dt_out("nsb", [2, 16, 64, 64])
    projT = nc.dram_tensor("projT", [PROJ_ROWS, G], F32, kind=scr_kind).ap()
    mixT = nc.dram_tensor("mixT", [D, G], BF16, kind=scr_kind).ap()
    x1s = nc.dram_tensor("x1s", [G, D], F32, kind=scr_kind).ap()
    hT = nc.dram_tensor("hT", [D, G], BF16, kind="Internal").ap()
    wg16 = nc.dram_tensor("wg16", [22, 128, 16 * 256], BF16, kind="Internal").ap()
    wu16 = nc.dram_tensor("wu16", [22, 128, 16 * 256], BF16, kind="Internal").ap()
    wd16 = nc.dram_tensor("wd16", [16, 128, 44 * 128], BF16, kind="Internal").ap()
    if debug:
        dbg_mod = dt_out("dbg_mod", [128, 96, 2])
        dbg_mix = dt_out("dbg_mix", [D, G])

    with ExitStack() as st:
        P = Prog(nc, st)
        big = st.enter_context(nc.sbuf_tensor("big", [128, SB_WORDS], F32))
        psum = st.enter_context(nc.psum_tensor("psum", [128, 8, 512], F32))
        sb = SBAlloc(big, SB_WORDS)
        bigr = st.enter_context(nc.sbuf_tensor("bigr", [128, SBR_WORDS], TDT))
        rtiles = [Tl(bigr[:, i * 512:(i + 1) * 512]) for i in range(12)]
        banks = Ring([Tl(psum[:, i, :], Res(excl=True)) for i in range(8)])

        cst = sb.t([NCONST])
        P.dma(cst.ap, consts, writes=[cst.r])
        cf = lambda name, n: cst.ap[:, CO[name]:CO[name] + n]
        ident = cf("ident", 128)
        ones_f = cf("ones", 128)
        blk_f = cf("blk", 128)
        vec = sb.t([NVEC])
        P.dma(vec.ap, vecs, writes=[vec.r])
        vcol = lambda name, j=0: vec.ap[:, VO[name] + j:VO[name] + j + 1]
        cb = sb.t([NCB], BF16)
        P.copy("dve", cb.ap[:, 0:128], ident, [cst.r], [cb.r])
        P.copy("dve", cb.ap[:, 128:256], ones_f, [cst.r], [cb.r])
        for i, nm in enumerate(("m_su", "m_iu", "m_sl", "m_il")):
            P.copy("dve", cb.ap[:, 256 + 128 * i:384 + 128 * i], cf(nm, 128), [cst.r], [cb.r])
        for j, (ms_, mi_) in enumerate((("m_su", "m_iu"), ("m_sl", "m_il"))):
            o_ = 768 + 384 * j
            P.copy("dve", cb.ap[:, o_:o_ + 128], cf(ms_, 128), [cst.r], [cb.r])
            P.copy("dve", cb.ap[:, o_ + 128:o_ + 256], cf(mi_, 128), [cst.r], [cb.r])
            P.copy("dve", cb.ap[:, o_ + 256:o_ + 384], cf(mi_, 128), [cst.r], [cb.r])
        mask3 = [cb.ap[:, 768:1152], cb.ap[:, 1152:1536]]
        ident_b = cb.ap[:, 0:128]
        ones_b = cb.ap[:, 128:256]
        mask_b = {nm: cb.ap[:, 256 + 128 * i:384 + 128 * i] for i, nm in enumerate(("su", "iu", "sl", "il"))}
        dv = sb.t([64])
        P.ts(dv.ap[:, 0:26], vec.ap[:, VO["mu"]:VO["mu"] + 26], -1.0, 1.0, ALU.mult, ALU.add, [vec.r], [dv.r])
        P.ts(dv.ap[:, 26:52], vec.ap[:, VO["mu"]:VO["mu"] + 26], 0.5, None, ALU.mult, None, [vec.r], [dv.r])
        P.ts(dv.ap[:, 52:60], vec.ap[:, VO["k_a"]:VO["k_a"] + 8], -1.0, 1.0, ALU.mult, ALU.add, [vec.r], [dv.r])
        omm = lambda i: dv.ap[:, i:i + 1]
        hmu = lambda i: dv.ap[:, 26 + i:27 + i]
        omka = lambda i: dv.ap[:, 52 + i:53 + i]
        epsc = sb.t([8])
        P.call("dve", "memset", (epsc.ap[:, 0:1], RMS_EPS), [], [epsc.r])
        P.call("dve", "memset", (epsc.ap[:, 1:2], GN_EPS), [], [epsc.r])
        eps_rms = epsc.ap[:, 0:1]
        eps_gn = epsc.ap[:, 1:2]
        mod = sb.t([96, 2])
        mod1p = sb.t([32, 2])
        base_mark = sb.off

        if "A" in stages:
            cnd = sb.t([16, 2])
            bm = sb.t([96])
            P.dma(cnd.ap, condT, writes=[cnd.r])
            P.dma(bm.ap, bmodT, writes=[bm.r])
            scn = sb.t([16, 2])
            P.act(scn.ap, cnd.ap, AF.Silu, [cnd.r], [scn.r])
            wring = Ring(sb.ring(2, [16, 512]))
            wv = w_mod.rearrange("(k p) c -> p k c", p=128)
            mrow = sb.t([6 * D])
            for blk in range(24):
                wt = wring.next()
                P.dma(wt.ap, wv[:, :, blk * 512:(blk + 1) * 512], writes=[wt.r])
                bk = banks.next()
                for k in range(16):
                    P.mm(bk.ap[0:2, :], scn.ap[:, k, :], wt.ap[:, k, :], k == 0, k == 15, [wt.r, scn.r], [bk.r])
                P.copy(P.ev(), mrow.ap[0:2, blk * 512:(blk + 1) * 512], bk.ap[0:2, :], [bk.r], [mrow.r])
            bk = banks.next()
            for j in range(96):
                P.tr(bk.ap[:, 2 * j:2 * j + 2], mrow.ap[0:2, j * 128:(j + 1) * 128], ident[0:2, 0:2],
                     [mrow.r, cst.r], [bk.r], inc=(j == 95))
            P.tt(mod.ap, bk.ap[:, 0:192].rearrange("p (j c) -> p j c", c=2),
                 bm.ap.unsqueeze(2).to_broadcast([128, 96, 2]), ALU.add, [bk.r, bm.r], [mod.r])
            P.ts(mod1p.ap[:, 0:16, :], mod.ap[:, 16:32, :], 1.0, None, ALU.add, None, [mod.r], [mod1p.r])
            P.ts(mod1p.ap[:, 16:32, :], mod.ap[:, 64:80, :], 1.0, None, ALU.add, None, [mod.r], [mod1p.r])
            if debug:
                P.dma(dbg_mod, mod.ap, reads=[mod.r], q="act", key="st")
            P.barrier()
            sb.off = base_mark

        cond_of_tile = lambda tt: 0 if tt < NS // 128 else 1

        if "B" in stages:
            xmT = sb.a([16, G], BF16)
            xm_r = [Res() for _ in range(G // 512)]
            xring = Ring(sb.ring(2, [D]))
            for tt in range(G // 128):
                xt = xring.next()
                P.dma(xt.ap, x_all[tt * 128:(tt + 1) * 128, :], writes=[xt.r])
                cnd_i = cond_of_tile(tt)
                for q4 in range(4):
                    bk = banks.next()
                    for i in range(4):
                        k = q4 * 4 + i
                        P.tr(bk.ap[:, i * 128:(i + 1) * 128], xt.ap[:, k * 128:(k + 1) * 128], ident,
                             [xt.r, cst.r], [bk.r], inc=(i == 3))
                    eng_ = P.ev()
                    for i in range(4):
                        k = q4 * 4 + i
                        P.affine(eng_, xmT[:, k, tt * 128:(tt + 1) * 128], bk.ap[:, i * 128:(i + 1) * 128],
                                 mod1p.ap[:, k, cnd_i:cnd_i + 1], mod.ap[:, k, cnd_i:cnd_i + 1],
                                 [bk.r, mod.r, mod1p.r], [xm_r[tt // 4]])
            wring = Ring(sb.ring(3, [16, 256], BF16))
            oring = Ring(sb.ring(4, [512]))
            wv = w_in.rearrange("(k p) c -> p k c", p=128)
            for wb in range(17):
                wt = wring.next()
                if wb < 16:
                    P.dma(wt.ap, wv[:, :, wb * 256:(wb + 1) * 256], writes=[wt.r], q="pool", key="ldw")
                    nsub = 2
                else:
                    P.dma(wt.ap[:, :, 0:64], wv[:, :, 4096:4160], writes=[wt.r], q="pool", key="ldw")
                    P.dma(wt.ap[:, :, 64:96], wv[:, :, 800:832], writes=[wt.r], q="pool", key="ldw")
                    P.dma(wt.ap[:, :, 96:128], wv[:, :, 768:800], writes=[wt.r], q="pool", key="ldw")
                    nsub = 1
                for sub in range(nsub):
                    row0 = wb * 256 + sub * 128
                    for tb in range(G // 512):
                        bk = banks.next()
                        for k in range(16):
                            P.mm(bk.ap, wt.ap[:, k, sub * 128:(sub + 1) * 128], xmT[:, k, tb * 512:(tb + 1) * 512],
                                 k == 0, k == 15, [wt.r, xm_r[tb]], [bk.r])
                        ot = oring.next()
                        P.copy(P.ev(), ot.ap, bk.ap, [bk.r], [ot.r])
                        P.dma(projT[row0:row0 + 128, tb * 512:(tb + 1) * 512], ot.ap, reads=[ot.r], q="act", key="st")
            P.barrier()
            sb.off = base_mark

        seqs = [(0, NS, True, 0), (NS, NP_, False, 0), (NS + NP_, NP_, False, 1)]
        if dbg_seqs is not None:
            seqs = [seqs[i] for i in dbg_seqs]
        pair_list = list(range(8)) if dbg_pairs is None else list(dbg_pairs)

        if "C" in stages:
            cgroups = [(0, NS, 1, True, 0), (NS, NP_, 2, False, 0)]
            if dbg_seqs is not None:
                cgroups = [cgroups[i] for i in sorted(set(min(i, 1) for i in dbg_seqs))]
            for (t0, Sq, nseq, samp, pi) in cgroups:
                mark = sb.off
                S = Sq * nseq
                NK = S + (256 if samp else 0)
                nkt = NK // 128
                ablocks = []
                for q_ in range(nseq):
                    kts = list(range(q_ * Sq // 128, (q_ + 1) * Sq // 128)) + list(range(S // 128, nkt))
                    for o_ in range(q_ * Sq, (q_ + 1) * Sq, 512):
                        ablocks.append((o_, min(512, (q_ + 1) * Sq - o_), kts))
                tbs = [(o, min(512, S - o)) for o in range(0, S, 512)]
                kbs = [(o, min(512, NK - o)) for o in range(0, NK, 512)]
                stg = Ring(sb.ring(2, [S]))
                ostg = Ring(sb.ring(2, [320]))
                qdg = sb.t([4, S], BF16)
                sq = sb.t([4, S], BF16)
                rq = sb.t([S])
                rkv = sb.t([S])
                ckv_all = sb.t([2, NK], BF16)
                kr_all = sb.t([NK], BF16)
                if samp:
                    rope = sb.t([2, NS])
                    P.dma(rope.ap[0:64], ropeT, writes=[rope.r])
                for k in range(4):
                    s_ = stg.next()
                    P.dma(s_.ap, projT[k * 128:(k + 1) * 128, t0:t0 + S], writes=[s_.r])
                    P.act(sq.ap[:, k, :], s_.ap, AF.Square, [s_.r], [sq.r])
                    P.ts(qdg.ap[:, k, :], s_.ap, vcol("qg", k), None, ALU.mult, None, [s_.r, vec.r], [qdg.r])
                for (o, n) in tbs:
                    bk = banks.next()
                    for k in range(4):
                        P.mm(bk.ap[:, 0:n], ones_b, sq.ap[:, k, o:o + n], k == 0, k == 3, [cb.r, sq.r], [bk.r])
                    P.act(rq.ap[:, o:o + n], bk.ap[:, 0:n], AF.Sqrt, [bk.r, epsc.r], [rq.r], scale=1.0 / 512, bias=eps_rms)
                P.call("dve", "reciprocal", (rq.ap, rq.ap), [rq.r], [rq.r])
                ckf = sb.t([2, S])
                for k in range(2):
                    s_ = stg.next()
                    P.dma(s_.ap, projT[OFF_KV + k * 128:OFF_KV + (k + 1) * 128, t0:t0 + S], writes=[s_.r])
                    P.act(sq.ap[:, k, :], s_.ap, AF.Square, [s_.r], [sq.r])
                    P.ts(ckf.ap[:, k, :], s_.ap, vcol("kvg", k), None, ALU.mult, None, [s_.r, vec.r], [ckf.r])
                for (o, n) in tbs:
                    bk = banks.next()
                    for k in range(2):
                        P.mm(bk.ap[:, 0:n], ones_b, sq.ap[:, k, o:o + n], k == 0, k == 1, [cb.r, sq.r], [bk.r])
                    P.act(rkv.ap[:, o:o + n], bk.ap[:, 0:n], AF.Sqrt, [bk.r, epsc.r], [rkv.r], scale=1.0 / 256, bias=eps_rms)
                P.call("dve", "reciprocal", (rkv.ap, rkv.ap), [rkv.r], [rkv.r])
                for k in range(2):
                    P.tt(ckf.ap[:, k, :], ckf.ap[:, k, :], rkv.ap, ALU.mult, [ckf.r, rkv.r], [ckf.r])
                    P.copy("act", ckv_all.ap[:, k, 0:S], ckf.ap[:, k, :], [ckf.r], [ckv_all.r])
                krf = sb.t([S], parts=(0, 64))
                P.dma(krf.ap, projT[OFF_KR:OFF_KR + 64, t0:t0 + S], writes=[krf.r])
                if samp:
                    krs = sb.t([S], parts=(0, 64))
                    P.dma(krs.ap, projT[4160:4224, t0:t0 + S], writes=[krs.r])
                    P.tt(krf.ap, krf.ap, rope.ap[0:64, 0, :], ALU.mult, [krf.r, rope.r], [krf.r])
                    P.tt(krs.ap, krs.ap, rope.ap[0:64, 1, :], ALU.mult, [krs.r, rope.r], [krs.r])
                    P.tt(kr_all.ap[0:64, 0:S], krf.ap, krs.ap, ALU.add, [krf.r, krs.r], [kr_all.r])
                    cc = sb.t([2, 256])
                    P.dma(cc.ap, cache_ckv.rearrange("(t p) f -> p t f", p=128), writes=[cc.r])
                    ck = sb.t([2, 64])
                    P.dma(ck.ap, cache_kr.rearrange("(t p) f -> p t f", p=128), writes=[ck.r])
                    for tt in range(2):
                        bk = banks.next()
                        for k in range(2):
                            P.tr(bk.ap[:, k * 128:(k + 1) * 128], cc.ap[:, tt, k * 128:(k + 1) * 128], ident,
                                 [cc.r, cst.r], [bk.r], inc=False)
                        P.tr(bk.ap[0:64, 256:384], ck.ap[:, tt, :], ident, [ck.r, cst.r], [bk.r])
                        eng_ = P.ev()
                        for k in range(2):
                            P.copy(eng_, ckv_all.ap[:, k, S + tt * 128:S + (tt + 1) * 128],
                                   bk.ap[:, k * 128:(k + 1) * 128], [bk.r], [ckv_all.r])
                        P.copy(eng_, kr_all.ap[0:64, S + tt * 128:S + (tt + 1) * 128], bk.ap[0:64, 256:384],
                               [bk.r], [kr_all.r])
                else:
                    P.copy("act", kr_all.ap[0:64, 0:S], krf.ap, [krf.r], [kr_all.r])
                    for tt in range(S // 128):
                        bk = banks.next()
                        for k in range(2):
                            P.tr(bk.ap[:, k * 128:(k + 1) * 128], ckf.ap[:, k, tt * 128:(tt + 1) * 128], ident,
                                 [ckf.r, cst.r], [bk.r], inc=False)
                        P.tr(bk.ap[:, 256:320], krf.ap[0:64, tt * 128:(tt + 1) * 128], ident[0:64, 0:64],
                             [krf.r, cst.r], [bk.r])
                        o_ = ostg.next()
                        P.copy(P.ev(), o_.ap[:, 0:320], bk.ap[:, 0:320], [bk.r], [o_.r])
                        r0 = pi * NP_ + tt * 128
                        P.dma(nckv[r0:r0 + 128, :], o_.ap[:, 0:256], reads=[o_.r], q="act", key="st")
                        P.dma(nkr[r0:r0 + 128, :], o_.ap[:, 256:320], reads=[o_.r], q="act", key="st")
                wq_ring = Ring(sb.ring(2, [4, 256], BF16))
                wk_ring = Ring(sb.ring(2, [2, 256], BF16))
                qn = sb.t([S], BF16)
                qr = sb.t([S], BF16)
                kn = sb.t([NK], BF16)
                vh = sb.t([nkt, 128], BF16)
                tmpr = Ring(sb.ring(2, [2, 512]))
                pring = Ring(sb.ring(2, [512], BF16))
                oring = Ring(sb.ring(2, [512], BF16))
                rring = Ring(sb.ring(1, [512]))
                uqv = w_uq.rearrange("(k p) c -> p k c", p=128)
                ukv = w_uk.rearrange("(k p) c -> p k c", p=128)
                uvv = w_uv.rearrange("(k p) c -> p k c", p=128)
                for h in range(8):
                    wq = wq_ring.next()
                    c0 = h * 192
                    P.dma(wq.ap[:, :, 0:192], uqv[:, :, c0:c0 + 192], writes=[wq.r], q="pool", key="ldw")
                    P.dma(wq.ap[:, :, 192:224], uqv[:, :, c0 + 160:c0 + 192], writes=[wq.r], q="pool", key="ldw")
                    P.dma(wq.ap[:, :, 224:256], uqv[:, :, c0 + 128:c0 + 160], writes=[wq.r], q="pool", key="ldw")
                    wk = wk_ring.next()
                    P.dma(wk.ap[:, :, 0:128], ukv[:, :, h * 128:(h + 1) * 128], writes=[wk.r], q="pool", key="ldw")
                    P.dma(wk.ap[:, :, 128:256], uvv[:, :, h * 128:(h + 1) * 128], writes=[wk.r], q="pool", key="ldw")
                    for (o, n) in tbs:
                        bk = banks.next()
                        for k in range(4):
                            P.mm(bk.ap[:, 0:n], wq.ap[:, k, 0:128], qdg.ap[:, k, o:o + n], k == 0, k == 3,
                                 [wq.r, qdg.r], [bk.r])
                        P.tt(qn.ap[:, o:o + n], bk.ap[:, 0:n], rq.ap[:, o:o + n], ALU.mult, [bk.r, rq.r], [qn.r])
                        bk = banks.next()
                        for k in range(4):
                            P.mm(bk.ap[0:64, 0:n], wq.ap[:, k, 128:192], qdg.ap[:, k, o:o + n], k == 0, k == 3,
                                 [wq.r, qdg.r], [bk.r])
                        if not samp:
                            P.tt(qr.ap[0:64, o:o + n], bk.ap[0:64, 0:n], rq.ap[0:64, o:o + n], ALU.mult,
                                 [bk.r, rq.r], [qr.r])
                        else:
                            bk2 = banks.next()
                            for k in range(4):
                                P.mm(bk2.ap[0:64, 0:n], wq.ap[:, k, 192:256], qdg.ap[:, k, o:o + n], k == 0, k == 3,
                                     [wq.r, qdg.r], [bk2.r])
                            tm = tmpr.next()
                            P.tt(tm.ap[0:64, 0, 0:n], bk.ap[0:64, 0:n], rope.ap[0:64, 0, o:o + n], ALU.mult,
                                 [bk.r, rope.r], [tm.r])
                            P.tt(tm.ap[0:64, 1, 0:n], bk2.ap[0:64, 0:n], rope.ap[0:64, 1, o:o + n], ALU.mult,
                                 [bk2.r, rope.r], [tm.r])
                            P.tt(tm.ap[0:64, 0, 0:n], tm.ap[0:64, 0, 0:n], tm.ap[0:64, 1, 0:n], ALU.add,
                                 [tm.r], [tm.r])
                            P.tt(qr.ap[0:64, o:o + n], tm.ap[0:64, 0, 0:n], rq.ap[0:64, o:o + n], ALU.mult,
                                 [tm.r, rq.r], [qr.r])
                    for (o, n) in kbs:
                        bk = banks.next()
                        for k in range(2):
                            P.mm(bk.ap[:, 0:n], wk.ap[:, k, 0:128], ckv_all.ap[:, k, o:o + n], k == 0, k == 1,
                                 [wk.r, ckv_all.r], [bk.r])
                        P.copy(P.ev(), kn.ap[:, o:o + n], bk.ap[:, 0:n], [bk.r], [kn.r])
                    for g0 in range(0, nkt, 4):
                        ng = min(4, nkt - g0)
                        bk = banks.next()
                        for i in range(ng):
                            kt = g0 + i
                            for k in range(2):
                                P.mm(bk.ap[:, i * 128:(i + 1) * 128], ckv_all.ap[:, k, kt * 128:(kt + 1) * 128],
                                     wk.ap[:, k, 128:256], k == 0, k == 1, [wk.r, ckv_all.r], [bk.r],
                                     inc=(k == 1 and i == ng - 1))
                        P.copy(P.ev(), vh.ap[:, g0:g0 + ng, :],
                               bk.ap[:, 0:ng * 128].rearrange("p (a b) -> p a b", a=ng), [bk.r], [vh.r])
                    for (o, n, kts) in ablocks:
                        bo = banks.take()
                        bd = banks.take()

                        def scores(kt):
                            bs = banks.next()
                            P.mm(bs.ap[:, 0:n], kn.ap[:, kt * 128:(kt + 1) * 128], qn.ap[:, o:o + n], True, False,
                                 [kn.r, qn.r], [bs.r])
                            P.mm(bs.ap[:, 0:n], kr_all.ap[0:64, kt * 128:(kt + 1) * 128], qr.ap[0:64, o:o + n],
                                 False, True, [kr_all.r, qr.r], [bs.r])
                            return bs
                        bs_next = scores(kts[0])
                        for ki, kt in enumerate(kts):
                            bs = bs_next
                            first, last = (ki == 0), (ki == len(kts) - 1)
                            if not last:
                                bs_next = scores(kts[ki + 1])
                            pt = pring.next()
                            P.act(pt.ap[:, 0:n], bs.ap[:, 0:n], AF.Exp, [bs.r], [pt.r], scale=SCALE)
                            P.mm(bo.ap[:, 0:n], vh.ap[:, kt, :], pt.ap[:, 0:n], first, last, [vh.r, pt.r], [bo.r])
                            P.mm(bd.ap[:, 0:n], ones_b, pt.ap[:, 0:n], first, last, [cb.r, pt.r], [bd.r])
                        rc = rring.next()
                        P.call("dve", "reciprocal", (rc.ap[:, 0:n], bd.ap[:, 0:n]), [bd.r], [rc.r])
                        ot = oring.next()
                        P.tt(ot.ap[:, 0:n], bo.ap[:, 0:n], rc.ap[:, 0:n], ALU.mult, [bo.r, rc.r], [ot.r])
                        P.dma(mixT[h * 128:(h + 1) * 128, t0 + o:t0 + o + n], ot.ap[:, 0:n], reads=[ot.r],
                              q="act", key="st")
                        banks.give(bo)
                        banks.give(bd)
                P.barrier()
                sb.off = mark

        if "D" in stages:
            upv = {0: (w_up_f, a_up_f, "w0f", "a0f"), 1: (w_up_b, a_up_b, "w0b", "a0b")}
            dgroups = [(0, NS, 1, True, 0), (NS, NP_, 2, False, 0)]
            if dbg_seqs is not None:
                dgroups = [dgroups[min(i, 1)] for i in sorted(set(min(i, 1) for i in dbg_seqs))]
            for (t0, Sq, nseq, samp, pi) in dgroups:
                mark = sb.off
                S = Sq * nseq
                NCH = S // C
                NCS = Sq // C
                tbs = [(o, min(512, S - o)) for o in range(0, S, 512)]
                RB = OFF_RW
                pool8 = [sb.t([S]) for _ in range(5)]
                free = list(pool8)
                raw_ring = Ring(sb.ring(1, [nseq * (Sq + 2)]))

                def shift_load(blk_i, out_t):
                    rw_ = raw_ring.next()
                    r0 = RB + blk_i * 128
                    r3 = rw_.ap.rearrange("p (q j) -> p q j", j=Sq + 2)
                    q3 = lambda ap: ap.rearrange("p (q j) -> p q j", j=Sq)
                    P.call("dve", "memset", (r3[:, :, 0:1], 0.0), [], [rw_.r])
                    P.call("dve", "memset", (r3[:, :, Sq + 1:Sq + 2], 0.0), [], [rw_.r])
                    P.dma(r3[:, :, 1:Sq + 1], projT[r0:r0 + 128, t0:t0 + S].rearrange("p (q j) -> p q j", j=Sq),
                          writes=[rw_.r])
                    tmp = free.pop()
                    P.tt(q3(tmp.ap), r3[:, :, 0:Sq], r3[:, :, 2:Sq + 2], ALU.add, [rw_.r], [tmp.r])
                    P.act(q3(out_t.ap), r3[:, :, 1:Sq + 1], AF.Identity, [rw_.r, dv.r], [out_t.r], scale=omm(blk_i))
                    P.stt(out_t.ap, tmp.ap, hmu(blk_i), out_t.ap, ALU.mult, ALU.add, [tmp.r, out_t.r, dv.r], [out_t.r])
                    free.append(tmp)

                twd = sb.t([S], PREC["lora"])
                sgd = sb.t([S], PREC["lora"])
                t_ = free.pop()
                shift_load(24, t_)
                P.act(twd.ap[0:64], t_.ap[0:64], AF.Tanh, [t_.r], [twd.r])
                P.copy("dve", twd.ap[64:128], t_.ap[64:128], [t_.r], [twd.r])
                shift_load(25, t_)
                P.act(sgd.ap, t_.ap, AF.Sigmoid, [t_.r], [sgd.r])
                free.append(t_)
                lw = sb.t([2, 1024], PREC["lora"])
                for d_ in (0, 1):
                    P.dma(lw.ap[0:64, d_, :], upv[d_][0], writes=[lw.r], q="pool", key="ldw")
                    P.dma(lw.ap[64:128, d_, :], upv[d_][1], writes=[lw.r], q="pool", key="ldw")
                gw = sb.t([1024], PREC["lora"])
                P.dma(gw.ap, g_up, writes=[gw.r], q="pool", key="ldw")
                r_t, v_t, kk_t, kds_t = (sb.t([S]) for _ in range(4))
                y_ap = kk_t.ap
                y_r = [Res() for _ in range(NCH)]
                g_t = sb.t([S], BF16)
                vtok = sb.t([NCH, 128], BF16)
                aq = [sb.t([NCH, 2, C], BF16) for _ in range(2)]
                bt = [sb.t([S], BF16) for _ in range(2)]
                kt_ = [sb.t([S], BF16) for _ in range(2)]
                gam = [sb.t([NCH]) for _ in range(2)]
                S32 = [[sb.t([64]) for _ in range(nseq)] for _ in range(2)]
                Sg = [[sb.t([64]) for _ in range(nseq)] for _ in range(2)]
                Sb_ = [[sb.t([64], BF16) for _ in range(nseq)] for _ in range(2)]
                mring = Ring(rtiles[0:8])
                pr_ring = Ring(rtiles[8:12])
                mbring = Ring(sb.ring(8, [512], BF16))
                pbring = Ring(sb.ring(4, [512], BF16))
                aring = [[Ring(sb.ring(2, [2, 384], BF16)) for _ in range(nseq)] for _ in range(2)]
                xring = [[Ring(sb.ring(2, [128], BF16)) for _ in range(nseq)] for _ in range(2)]
                uring = [[Ring(sb.ring(2, [128], BF16)) for _ in range(nseq)] for _ in range(2)]
                orng = Ring(sb.ring(1, [512], BF16))
                sto = Ring(sb.ring(2, [128]))
                stin = sb.t([2, 64], parts=(0, 64))
                pview = lambda bk_: bk_.ap[:, 0:256].bitcast(BF16)

                for hp in pair_list:
                    k_t = free.pop()
                    shift_load(hp, r_t)
                    shift_load(8 + hp, k_t)
                    shift_load(16 + hp, v_t)
                    sqt = free.pop()
                    P.ts(kk_t.ap, k_t.ap, vcol("k_k", hp), None, ALU.mult, None, [k_t.r, vec.r], [kk_t.r] + y_r)
                    P.act(sqt.ap, kk_t.ap, AF.Square, [kk_t.r], [sqt.r])
                    rs = free.pop()
                    for (o, n) in tbs:
                        bk = banks.next()
                        P.mm(bk.ap[:, 0:n], blk_f, sqt.ap[:, o:o + n], True, True, [cst.r, sqt.r], [bk.r])
                        P.ts(rs.ap[:, o:o + n], bk.ap[:, 0:n], 64.0, 1e-24, ALU.mult, ALU.max, [bk.r], [rs.r])
                    P.act(rs.ap, rs.ap, AF.Sqrt, [rs.r], [rs.r])
                    P.call("dve", "reciprocal", (rs.ap, rs.ap), [rs.r], [rs.r])
                    P.tt(kk_t.ap, kk_t.ap, rs.ap, ALU.mult, [kk_t.r, rs.r], [kk_t.r])
                    free.append(rs)
                    free.append(sqt)
                    for (o, n) in tbs:
                        bk = banks.next()
                        P.mm(bk.ap[:, 0:n], gw.ap[:, hp * 128:(hp + 1) * 128], sgd.ap[:, o:o + n], True, True,
                             [gw.r, sgd.r], [bk.r])
                        P.copy("act", g_t.ap[:, o:o + n], bk.ap[:, 0:n], [bk.r], [g_t.r])
                    vb = free.pop()
                    vbb = vb.ap.bitcast(BF16)[:, 0:S]
                    P.copy("act", vbb, v_t.ap, [v_t.r], [vb.r])
                    for c0 in range(0, NCH, 4):
                        ng = min(4, NCH - c0)
                        bk = banks.next()
                        pb = pview(bk)
                        for i in range(ng):
                            P.tr(pb[:, i * 128:(i + 1) * 128], vbb[:, (c0 + i) * C:(c0 + i + 1) * C], ident_b,
                                 [vb.r, cb.r], [bk.r], inc=(i == ng - 1))
                        P.copy(P.ev(), vtok.ap[:, c0:c0 + ng, :], pb[:, 0:ng * 128].rearrange("p (a b) -> p a b", a=ng),
                               [bk.r], [vtok.r])
                    free.append(vb)

                    for d_ in (0, 1):
                        rev = (d_ == 1)
                        V = (lambda ap: ap[:, ::-1]) if rev else (lambda ap: ap)
                        aq_d, bt_d, kt_d, gam_d = aq[d_], bt[d_], kt_[d_], gam[d_]
                        ld = free.pop()
                        a_ = free.pop()
                        for (o, n) in tbs:
                            bk = banks.next()
                            P.mm(bk.ap[:, 0:n], lw.ap[0:64, d_, hp * 128:(hp + 1) * 128], twd.ap[0:64, o:o + n],
                                 True, True, [lw.r, twd.r], [bk.r])
                            P.act(ld.ap[:, o:o + n], bk.ap[:, 0:n], AF.Sigmoid, [bk.r, vec.r], [ld.r],
                                  bias=vcol(upv[d_][2], hp))
                            bk = banks.next()
                            P.mm(bk.ap[:, 0:n], lw.ap[64:128, d_, hp * 128:(hp + 1) * 128], twd.ap[64:128, o:o + n],
                                 True, True, [lw.r, twd.r], [bk.r])
                            P.act(a_.ap[:, o:o + n], bk.ap[:, 0:n], AF.Sigmoid, [bk.r, vec.r], [a_.r],
                                  bias=vcol(upv[d_][3], hp))
                        kd = free.pop()
                        P.act(kd.ap, a_.ap, AF.Identity, [a_.r, vec.r, dv.r], [kd.r], scale=vcol("k_a", hp),
                              bias=omka(hp))
                        P.tt(kd.ap, kd.ap, k_t.ap, ALU.mult, [kd.r, k_t.r], [kd.r])
                        if d_ == 0:
                            P.copy("act", kds_t.ap, kd.ap, [kd.r], [kds_t.r])
                        else:
                            P.tt(kds_t.ap, kds_t.ap, kd.ap, ALU.add, [kds_t.r, kd.r], [kds_t.r])
                        cl = free.pop()
                        for cc_ in range(NCH):
                            cs = slice(cc_ * C, (cc_ + 1) * C)
                            P.call("dve", "tensor_tensor_scan", (V(cl.ap[:, cs]), ones_f, V(ld.ap[:, cs]), 0.0, ALU.mult, ALU.add),
                                   [ld.r, cst.r], [cl.r])
                        P.tt(ld.ap, cl.ap, ld.ap, ALU.subtract, [cl.r, ld.r], [ld.r])
                        P.act(ld.ap, ld.ap, AF.Exp, [ld.r], [ld.r], scale=NEG_EXP_HALF)
                        P.stt(aq_d.ap[:, :, 0, :], kk_t.ap.rearrange("p (c i) -> p c i", i=C), -1.0,
                              ld.ap.rearrange("p (c i) -> p c i", i=C), ALU.mult, ALU.mult, [kk_t.r, ld.r], [aq_d.r])
                        P.act(ld.ap, cl.ap, AF.Exp, [cl.r], [ld.r], scale=NEG_EXP_HALF)
                        P.tt(aq_d.ap[:, :, 1, :], r_t.ap.rearrange("p (c i) -> p c i", i=C),
                             ld.ap.rearrange("p (c i) -> p c i", i=C), ALU.mult, [r_t.r, ld.r], [aq_d.r])
                        gcol = 0 if rev else C - 1
                        P.copy("dve", gam_d.ap, ld.ap.rearrange("p (c i) -> p c i", i=C)[:, :, gcol], [ld.r], [gam_d.r])
                        P.act(cl.ap, cl.ap, AF.Exp, [cl.r], [cl.r], scale=-NEG_EXP_HALF)
                        P.tt(kt_d.ap, kd.ap, cl.ap, ALU.mult, [kd.r, cl.r], [kt_d.r])
                        P.tt(a_.ap, a_.ap, kk_t.ap, ALU.mult, [a_.r, kk_t.r], [a_.r])
                        P.tt(bt_d.ap, a_.ap, cl.ap, ALU.mult, [a_.r, cl.r], [bt_d.r])
                        for t_x in (cl, kd, a_, ld):
                            free.append(t_x)
                    free.append(k_t)

                    held = [free.pop() for _ in range(4)]
                    btok, ktok, Tt = [], [], []
                    for d_ in (0, 1):
                        p1, p2 = held[2 * d_], held[2 * d_ + 1]
                        h_ = S // 2
                        btok.append(Tl(p1.ap[:, 0:h_].bitcast(BF16).rearrange("p (c f) -> p c f", f=128), p1.r))
                        ktok.append(Tl(p1.ap[:, h_:S].bitcast(BF16).rearrange("p (c f) -> p c f", f=128), p1.r))
                        Tt.append(Tl(p2.ap.bitcast(BF16).rearrange("p (c h f) -> p c h f", h=2, f=128), p2.r))
                    for d_ in (0, 1):
                        rev = (d_ == 1)
                        aq_d, bt_d, kt_d = aq[d_], bt[d_], kt_[d_]
                        for (src_t, dst_t, eng_) in ((bt_d, btok[d_], "act"), (kt_d, ktok[d_], "dve")):
                            for c0 in range(0, NCH, 4):
                                ng = min(4, NCH - c0)
                                bk = banks.next()
                                pb = pview(bk)
                                for i in range(ng):
                                    P.tr(pb[:, i * 128:(i + 1) * 128], src_t.ap[:, (c0 + i) * C:(c0 + i + 1) * C], ident_b,
                                         [src_t.r, cb.r], [bk.r], inc=(i == ng - 1))
                                P.copy(eng_, dst_t.ap[:, c0:c0 + ng, :],
                                       pb[:, 0:ng * 128].rearrange("p (a b) -> p a b", a=ng), [bk.r], [dst_t.r])
                        mN = mask_b["sl"] if rev else mask_b["su"]
                        mNT = mask_b["su"] if rev else mask_b["sl"]
                        Tt_d = Tt[d_]
                        for c0 in range(0, NCH, 4):
                            nu = min(4, NCH - c0)
                            W_ = nu * 128
                            m4 = lambda ap: ap[:, 0:W_].rearrange("p (u f) -> p u f", u=nu)
                            bc4 = lambda ap: ap.unsqueeze(1).to_broadcast([128, nu, 128])
                            stt_ = {}
                            for hh in range(2):
                                p0 = hh * 64
                                bM, bMT = banks.next(), banks.next()
                                for u in range(nu):
                                    cc_ = c0 + u
                                    P.mm(bM.ap[:, u * 128:(u + 1) * 128], bt_d.ap[p0:p0 + 64, cc_ * C:(cc_ + 1) * C],
                                         aq_d.ap[p0:p0 + 64, cc_, 0, :], True, True, [bt_d.r, aq_d.r], [bM.r], inc=(u == nu - 1))
                                for u in range(nu):
                                    cc_ = c0 + u
                                    P.mm(bMT.ap[:, u * 128:(u + 1) * 128], aq_d.ap[p0:p0 + 64, cc_, 0, :],
                                         bt_d.ap[p0:p0 + 64, cc_ * C:(cc_ + 1) * C], True, True, [bt_d.r, aq_d.r], [bMT.r],
                                         inc=(u == nu - 1))
                                M, MT, Pm = mring.next(), mring.next(), pr_ring.next()
                                P.tt(m4(M.ap), m4(bM.ap), bc4(mN), ALU.mult, [bM.r, cb.r], [M.r])
                                P.tt(m4(MT.ap), m4(bMT.ap), bc4(mNT), ALU.mult, [bMT.r, cb.r], [MT.r])
                                P.tt(m4(Pm.ap), m4(M.ap), bc4(ident_b), ALU.add, [M.r, cb.r], [Pm.r])
                                stt_[hh] = [M, MT, Pm, None]
                            for lvl in range(1, 7):
                                last = (lvl == 6)
                                lowp = (lvl >= TSW)
                                nlow = (lvl + 1 >= TSW)
                                nb = {}
                                for hh in range(2):
                                    M, MT, Pm, Pb = stt_[hh]
                                    bMT2 = banks.next()
                                    for u in range(nu):
                                        sl = slice(u * 128, (u + 1) * 128)
                                        P.mm(bMT2.ap[:, sl], M.ap[:, sl], MT.ap[:, sl], True, True, [M.r, MT.r], [bMT2.r],
                                             inc=(u == nu - 1))
                                    bM2 = None
                                    if not last:
                                        bM2 = banks.next()
                                        for u in range(nu):
                                            sl = slice(u * 128, (u + 1) * 128)
                                            P.mm(bM2.ap[:, sl], MT.ap[:, sl], M.ap[:, sl], True, True, [M.r, MT.r],
                                                 [bM2.r], inc=(u == nu - 1))
                                    nb[hh] = (bMT2, bM2)
                                nm = {}
                                for hh in range(2):
                                    bMT2, bM2 = nb[hh]
                                    MT2 = (mbring if lowp else mring).next()
                                    P.copy("act", MT2.ap[:, 0:W_], bMT2.ap[:, 0:W_], [bMT2.r], [MT2.r])
                                    M2n, MT2n = None, MT2
                                    if not last:
                                        if nlow and not lowp:
                                            MT2n = mbring.next()
                                            P.copy("act", MT2n.ap[:, 0:W_], bMT2.ap[:, 0:W_], [bMT2.r], [MT2n.r])
                                        M2n = (mbring if nlow else mring).next()
                                        P.copy("dve", M2n.ap[:, 0:W_], bM2.ap[:, 0:W_], [bM2.r], [M2n.r])
                                    nm[hh] = (M2n, MT2n, MT2)
                                bPs = {}
                                for hh in range(2):
                                    M2n, MT2n, MT2 = nm[hh]
                                    Pm, Pb = stt_[hh][2], stt_[hh][3]
                                    bP = banks.next()
                                    rhsP = Pb if lowp else Pm
                                    for u in range(nu):
                                        sl = slice(u * 128, (u + 1) * 128)
                                        P.mm(bP.ap[:, sl], MT2.ap[:, sl], rhsP.ap[:, sl], True, True, [MT2.r, rhsP.r], [bP.r],
                                             inc=(u == nu - 1))
                                    bPs[hh] = bP
                                for hh in range(2):
                                    M2n, MT2n, MT2 = nm[hh]
                                    Pm = stt_[hh][2]
                                    bP = bPs[hh]
                                    if last:
                                        P.tt(Tt_d.ap[:, c0:c0 + nu, hh, :], m4(bP.ap), m4(Pm.ap), ALU.add, [bP.r, Pm.r], [Tt_d.r])
                                    else:
                                        Pn = pr_ring.next()
                                        P.tt(Pn.ap[:, 0:W_], bP.ap[:, 0:W_], Pm.ap[:, 0:W_], ALU.add, [bP.r, Pm.r], [Pn.r])
                                        Pbn = None
                                        if nlow:
                                            Pbn = pbring.next()
                                            P.copy("act", Pbn.ap[:, 0:W_], Pn.ap[:, 0:W_], [Pn.r], [Pbn.r])
                                        stt_[hh] = [M2n, MT2n, Pn, Pbn]

                    for d_ in (0, 1):
                        for q_ in range(nseq):
                            if samp:
                                src = (st_b if d_ == 1 else st_f)[2 * hp:2 * hp + 2].rearrange("h v k -> v h k")
                                P.dma(stin.ap, src, writes=[stin.r])
                                bk = banks.next()
                                P.tr(bk.ap[:, 0:64], stin.ap.rearrange("p h k -> p (h k)"), ident[0:64, 0:64],
                                     [stin.r, cst.r], [bk.r])
                                P.copy("dve", S32[d_][q_].ap, bk.ap[:, 0:64], [bk.r], [S32[d_][q_].r])
                            else:
                                P.call("dve", "memset", (S32[d_][q_].ap, 0.0), [], [S32[d_][q_].r])
                            P.copy("act", Sb_[d_][q_].ap, S32[d_][q_].ap, [S32[d_][q_].r], [Sb_[d_][q_].r])
                    P.call("dve", "memset", (y_ap, 0.0), [], [kk_t.r] + y_r)

                    def chunk_step(d_, q_, cc_):
                        rev = (d_ == 1)
                        aq_d, bt_d, kt_d, gam_d = aq[d_], bt[d_], kt_[d_], gam[d_]
                        btok_d, ktok_d, Tt_d = btok[d_], ktok[d_], Tt[d_]
                        S32_d, Sg_d, Sb_d = S32[d_][q_], Sg[d_][q_], Sb_[d_][q_]
                        m_s = mask_b["sl"] if rev else mask_b["su"]
                        m_i = mask_b["il"] if rev else mask_b["iu"]
                        cs = slice(cc_ * C, (cc_ + 1) * C)
                        A = aring[d_][q_].next()
                        for hh in range(2):
                            p0 = hh * 64
                            bA = banks.next()
                            P.mm(bA.ap[:, 0:256], kt_d.ap[p0:p0 + 64, cs],
                                 aq_d.ap[p0:p0 + 64, cc_, :, :].rearrange("p a b -> p (a b)"), True, True,
                                 [kt_d.r, aq_d.r], [bA.r], inc=False)
                            P.mm(bA.ap[:, 256:384], bt_d.ap[p0:p0 + 64, cs], aq_d.ap[p0:p0 + 64, cc_, 1, :], True, True,
                                 [bt_d.r, aq_d.r], [bA.r], inc=True)
                            P.tt(A.ap[:, hh, :], bA.ap[:, 0:384], mask3[d_], ALU.mult, [bA.r, cb.r], [A.r])
                        P.act(Sg_d.ap, S32_d.ap, AF.Copy, [S32_d.r, gam_d.r], [Sg_d.r], scale=gam_d.ap[:, cc_:cc_ + 1])
                        bX = banks.next()
                        for hh in range(2):
                            p0 = hh * 64
                            P.mm(bX.ap[:, hh * 64:(hh + 1) * 64], aq_d.ap[p0:p0 + 64, cc_, 0, :], Sb_d.ap[p0:p0 + 64, :],
                                 True, False, [aq_d.r, Sb_d.r], [bX.r], inc=False)
                            P.mm(bX.ap[:, hh * 64:(hh + 1) * 64], A.ap[:, hh, 0:128], vtok.ap[:, cc_, p0:p0 + 64],
                                 False, True, [A.r, vtok.r], [bX.r], inc=(hh == 1))
                        Xb = xring[d_][q_].next()
                        P.copy("act", Xb.ap, bX.ap[:, 0:128], [bX.r], [Xb.r])
                        bU = banks.next()
                        for hh in range(2):
                            P.mm(bU.ap[:, hh * 64:(hh + 1) * 64], Tt_d.ap[:, cc_, hh, :], Xb.ap[:, hh * 64:(hh + 1) * 64],
                                 True, True, [Tt_d.r, Xb.r], [bU.r], inc=(hh == 1))
                        Ub = uring[d_][q_].next()
                        P.copy("act", Ub.ap, bU.ap[:, 0:128], [bU.r], [Ub.r])
                        bS = banks.next()
                        for hh in range(2):
                            p0 = hh * 64
                            P.mm(bS.ap[p0:p0 + 64, 0:64], btok_d.ap[:, cc_, p0:p0 + 64], Ub.ap[:, p0:p0 + 64],
                                 True, False, [btok_d.r, Ub.r], [bS.r], inc=False)
                            P.mm(bS.ap[p0:p0 + 64, 0:64], ktok_d.ap[:, cc_, p0:p0 + 64], vtok.ap[:, cc_, p0:p0 + 64],
                                 False, True, [ktok_d.r, vtok.r], [bS.r], inc=(hh == 1))
                        bY = banks.next()
                        for hh in range(2):
                            p0 = hh * 64
                            P.mm(bY.ap[p0:p0 + 64, 0:128], Sb_d.ap[p0:p0 + 64, :], aq_d.ap[p0:p0 + 64, cc_, 1, :],
                                 True, False, [Sb_d.r, aq_d.r], [bY.r], inc=False)
                            P.mm(bY.ap[p0:p0 + 64, 0:128], Ub.ap[:, p0:p0 + 64], A.ap[:, hh, 256:384],
                                 False, False, [Ub.r, A.r], [bY.r], inc=False)
                            P.mm(bY.ap[p0:p0 + 64, 0:128], vtok.ap[:, cc_, p0:p0 + 64], A.ap[:, hh, 128:256],
                                 False, True, [vtok.r, A.r], [bY.r], inc=(hh == 1))
                        P.stt(S32_d.ap, bS.ap[:, 0:64], gam_d.ap[:, cc_:cc_ + 1], Sg_d.ap, ALU.mult, ALU.add,
                              [bS.r, gam_d.r, Sg_d.r], [S32_d.r])
                        P.copy("act", Sb_d.ap, S32_d.ap, [S32_d.r], [Sb_d.r])
                        P.tt(y_ap[:, cs], y_ap[:, cs], bY.ap[:, 0:128], ALU.add, [bY.r, y_r[cc_]], [y_r[cc_]])

                    for i in range(NCS):
                        for q_ in range(nseq):
                            chunk_step(0, q_, q_ * NCS + i)
                            chunk_step(1, q_, q_ * NCS + NCS - 1 - i)
                    if not samp:
                        for d_ in (0, 1):
                            for q_ in range(nseq):
                                bk = banks.next()
                                P.tr(bk.ap[0:64, 0:128], S32[d_][q_].ap, ident, [S32[d_][q_].r, cst.r], [bk.r])
                                so = sto.next()
                                P.copy("dve", so.ap[0:64, :], bk.ap[0:64, 0:128], [bk.r], [so.r])
                                dst = (nsb if d_ == 1 else nsf)[pi + q_, 2 * hp:2 * hp + 2].rearrange("h v k -> v h k")
                                P.dma(dst, so.ap[0:64, :].rearrange("p (h k) -> p h k", h=2), reads=[so.r], q="act", key="st")
                    for t_x in held:
                        free.append(t_x)
                    P.stt(kds_t.ap, r_t.ap, vcol("r_k", hp), kds_t.ap, ALU.mult, ALU.mult, [r_t.r, kds_t.r, vec.r],
                          [kds_t.r])
                    e1, e2, e3 = free.pop(), free.pop(), free.pop()
                    for (o, n) in tbs:
                        sl = slice(o, o + n)
                        b1, b2, b3 = banks.next(), banks.next(), banks.next()
                        P.mm(b1.ap[:, 0:n], blk_f, kds_t.ap[:, sl], True, True, [cst.r, kds_t.r], [b1.r])
                        P.mm(b2.ap[:, 0:n], blk_f, y_ap[:, sl], True, True, [cst.r] + y_r, [b2.r])
                        s1 = Tl(e1.ap[:, sl], e1.r)
                        P.act(s1.ap[:, 0:n], y_ap[:, sl], AF.Square, y_r, [s1.r])
                        P.mm(b3.ap[:, 0:n], blk_f, s1.ap[:, 0:n], True, True, [cst.r, s1.r], [b3.r])
                        s2 = Tl(e2.ap[:, sl], e2.r)
                        P.stt(s2.ap[:, 0:n], b1.ap[:, 0:n], 64.0, v_t.ap[:, sl], ALU.mult, ALU.mult, [b1.r, v_t.r], [s2.r])
                        mu_ = Tl(e3.ap[:, sl], e3.r)
                        P.copy("act", mu_.ap[:, 0:n], b2.ap[:, 0:n], [b2.r], [mu_.r])
                        P.tt(s1.ap[:, 0:n], mu_.ap[:, 0:n], mu_.ap[:, 0:n], ALU.mult, [mu_.r], [s1.r])
                        P.tt(s1.ap[:, 0:n], b3.ap[:, 0:n], s1.ap[:, 0:n], ALU.subtract, [b3.r, s1.r], [s1.r])
                        P.act(s1.ap[:, 0:n], s1.ap[:, 0:n], AF.Sqrt, [s1.r, epsc.r], [s1.r], bias=eps_gn)
                        P.call("dve", "reciprocal", (s1.ap[:, 0:n], s1.ap[:, 0:n]), [s1.r], [s1.r])
                        P.tt(mu_.ap[:, 0:n], y_ap[:, sl], mu_.ap[:, 0:n], ALU.subtract, y_r + [mu_.r], [mu_.r])
                        P.tt(mu_.ap[:, 0:n], mu_.ap[:, 0:n], s1.ap[:, 0:n], ALU.mult, [mu_.r, s1.r], [mu_.r])
                        P.affine("act", mu_.ap[:, 0:n], mu_.ap[:, 0:n], vcol("gn_g", hp), vcol("gn_b", hp),
                                 [mu_.r, vec.r], [mu_.r])
                        P.tt(mu_.ap[:, 0:n], mu_.ap[:, 0:n], s2.ap[:, 0:n], ALU.add, [mu_.r, s2.r], [mu_.r])
                        ot = orng.next()
                        P.tt(ot.ap[:, 0:n], mu_.ap[:, 0:n], g_t.ap[:, sl], ALU.mult, [mu_.r, g_t.r], [ot.r])
                        P.dma(mixT[1024 + hp * 128:1024 + (hp + 1) * 128, t0 + o:t0 + o + n], ot.ap[:, 0:n],
                              reads=[ot.r], q="act", key="st")
                    for t_x in (e1, e2, e3):
                        free.append(t_x)
                P.barrier()
                sb.off = mark

        def layer_norm_tile(y, stats_t, gbc, bbc, out_ap, reads, out_res):
            for k in range(4):
                P.call("dve", "bn_stats", (stats_t.ap[:, 6 * k:6 * k + 6], y.ap[:, k * 512:(k + 1) * 512]), [y.r], [stats_t.r])
            P.call("dve", "bn_aggr", (stats_t.ap[:, 24:26], stats_t.ap[:, 0:24]), [stats_t.r], [stats_t.r])
            P.ts(stats_t.ap[:, 26:27], stats_t.ap[:, 25:26], LN_EPS, None, ALU.add, None, [stats_t.r], [stats_t.r])
            P.act(stats_t.ap[:, 26:27], stats_t.ap[:, 26:27], AF.Sqrt, [stats_t.r], [stats_t.r])
            P.call("dve", "reciprocal", (stats_t.ap[:, 26:27], stats_t.ap[:, 26:27]), [stats_t.r], [stats_t.r])
            P.stt(stats_t.ap[:, 27:28], stats_t.ap[:, 24:25], -1.0, stats_t.ap[:, 26:27], ALU.mult, ALU.mult,
                  [stats_t.r], [stats_t.r])
            P.act(y.ap, y.ap, AF.Identity, [y.r, stats_t.r], [y.r], scale=stats_t.ap[:, 26:27], bias=stats_t.ap[:, 27:28])
            P.tt(y.ap, y.ap, gbc.ap, ALU.mult, [y.r, gbc.r], [y.r])
            P.tt(out_ap, y.ap, bbc.ap, ALU.add, [y.r, bbc.r] + reads, [out_res])

        if "E" in stages:
            mark = sb.off
            wo = sb.t([16, D], BF16)
            wov = w_out.rearrange("(k p) c -> p k c", p=128)
            for k4 in range(4):
                P.dma(wo.ap[:, k4 * 4:(k4 + 1) * 4, :], wov[:, k4 * 4:(k4 + 1) * 4, :], writes=[wo.r], q="pool", key="ldw")
            lg = sb.t([D]); lb = sb.t([D])
            P.dma(lg.ap, lnv[0:1, :].partition_broadcast(128), writes=[lg.r])
            P.dma(lb.ap, lnv[1:2, :].partition_broadcast(128), writes=[lb.r])
            g1bc = [sb.t([D]), sb.t([D])]
            dg = Ring(sb.ring(2, [128]))
            for cnd_i in range(2):
                for k in range(16):
                    d_t = dg.next()
                    P.ts(d_t.ap, ident, mod.ap[:, 32 + k, cnd_i:cnd_i + 1], None, ALU.mult, None, [cst.r, mod.r], [d_t.r])
                    bk = banks.next()
                    P.mm(bk.ap[:, 0:128], ones_f, d_t.ap, True, True, [cst.r, d_t.r], [bk.r])
                    P.copy(P.ev(), g1bc[cnd_i].ap[:, k * 128:(k + 1) * 128], bk.ap[:, 0:128], [bk.r], [g1bc[cnd_i].r])
            mxr = Ring(sb.ring(2, [16, 128], BF16))
            xr = Ring(sb.ring(2, [D]))
            yr = Ring(sb.ring(2, [D]))
            x1r = Ring(sb.ring(2, [D]))
            hr = Ring(sb.ring(2, [16, 128], BF16))
            stt_ = Ring(sb.ring(2, [32]))
            def e_mm(tt):
                ts_ = slice(tt * 128, (tt + 1) * 128)
                cnd_i = cond_of_tile(tt)
                mx = mxr.next()
                P.dma(mx.ap, mixT[:, ts_].rearrange("(k p) t -> p k t", p=128), writes=[mx.r])
                xt = xr.next()
                P.dma(xt.ap, x_all[ts_, :], writes=[xt.r])
                y = yr.next()
                bks = []
                for cbk in range(4):
                    bk = banks.next()
                    for k in range(16):
                        P.mm(bk.ap, mx.ap[:, k, :], wo.ap[:, k, cbk * 512:(cbk + 1) * 512], k == 0, k == 15,
                             [mx.r, wo.r], [bk.r])
                    bks.append(bk)

                def evac():
                    for cbk in range(4):
                        bk = bks[cbk]
                        P.tt(y.ap[:, cbk * 512:(cbk + 1) * 512], bk.ap, g1bc[cnd_i].ap[:, cbk * 512:(cbk + 1) * 512],
                             ALU.mult, [bk.r, g1bc[cnd_i].r], [y.r])
                    P.stt(y.ap, xt.ap, ALPHA, y.ap, ALU.mult, ALU.add, [xt.r, y.r], [y.r])
                return y, evac

            def e_ln(tt, y):
                ts_ = slice(tt * 128, (tt + 1) * 128)
                cnd_i = cond_of_tile(tt)
                x1 = x1r.next()
                s_ = stt_.next()
                layer_norm_tile(y, s_, lg, lb, x1.ap, [], x1.r)
                P.dma(x1s[ts_, :], x1.ap, reads=[x1.r], q="act", key="st")
                ht = hr.next()
                for q4 in range(4):
                    bk = banks.next()
                    for i in range(4):
                        k = q4 * 4 + i
                        P.tr(bk.ap[:, i * 128:(i + 1) * 128], x1.ap[:, k * 128:(k + 1) * 128], ident,
                             [x1.r, cst.r], [bk.r], inc=(i == 3))
                    eng_ = P.ev()
                    for i in range(4):
                        k = q4 * 4 + i
                        P.affine(eng_, ht.ap[:, k, :], bk.ap[:, i * 128:(i + 1) * 128],
                                 mod1p.ap[:, 16 + k, cnd_i:cnd_i + 1], mod.ap[:, 48 + k, cnd_i:cnd_i + 1],
                                 [bk.r, mod.r, mod1p.r], [ht.r])
                P.dma(hT[:, ts_].rearrange("(k p) t -> p k t", p=128), ht.ap, reads=[ht.r], q="act", key="st")

            y_next, ev_next = e_mm(0)
            ev_next()
            for tt in range(G // 128):
                y_cur = y_next
                if tt + 1 < G // 128:
                    y_next, ev_next = e_mm(tt + 1)
                e_ln(tt, y_cur)
                if tt + 1 < G // 128:
                    ev_next()
            P.barrier()
            sb.off = mark

        if "F" in stages:
            mark = sb.off
            lg = sb.t([D]); lb = sb.t([D])
            P.dma(lg.ap, lnv[2:3, :].partition_broadcast(128), writes=[lg.r])
            P.dma(lb.ap, lnv[3:4, :].partition_broadcast(128), writes=[lb.r])
            actT = sb.t([44, 512], BF16)
            f2 = sb.t([16, 512])
            hb_alias = Tl(f2.ap.rearrange("p k t -> p (k t)")[:, 0:4096].bitcast(BF16).rearrange("p (k t) -> p k t", k=16), f2.r)
            hring = Ring([hb_alias])
            wg_r = Ring(sb.ring(2, [16, 256], BF16))
            wu_r = Ring(sb.ring(2, [16, 256], BF16))
            wd_r = Ring(sb.ring(2, [44, 128], BF16))
            sgr = Ring(sb.ring(2, [512]))
            x1r = Ring(sb.ring(1, [D]))
            yr = Ring(sb.ring(1, [D]))
            stt_ = Ring(sb.ring(2, [32]))
            wgv = w_gate.rearrange("(k p) c -> p k c", p=128)
            wuv = w_upf.rearrange("(k p) c -> p k c", p=128)
            wdv = w_down.rearrange("(k p) c -> p k c", p=128)
            for tb in range(G // 512):
                cnd_i = 0 if tb < NS // 512 else 1
                tsl = slice(tb * 512, (tb + 1) * 512)
                hb = hring.next()
                P.dma(hb.ap, hT[:, tsl].rearrange("(k p) t -> p k t", p=128), writes=[hb.r])
                for fb in range(22):
                    wg, wu = wg_r.next(), wu_r.next()
                    if tb == 0:
                        P.dma(wg.ap, wgv[:, :, fb * 256:(fb + 1) * 256], writes=[wg.r], q="pool", key="ldw")
                        P.dma(wu.ap, wuv[:, :, fb * 256:(fb + 1) * 256], writes=[wu.r], q="pool", key="ldw")
                        P.dma(wg16[fb], wg.ap.rearrange("p k c -> p (k c)"), reads=[wg.r], q="sp", key="st")
                        P.dma(wu16[fb], wu.ap.rearrange("p k c -> p (k c)"), reads=[wu.r], q="sp", key="st")
                    else:
                        P.dma(wg.ap.rearrange("p k c -> p (k c)"), wg16[fb], writes=[wg.r], q="pool", key="ldw")
                        P.dma(wu.ap.rearrange("p k c -> p (k c)"), wu16[fb], writes=[wu.r], q="pool", key="ldw")
                    for sub in range(2):
                        f = fb * 2 + sub
                        bg, bu = banks.next(), banks.next()
                        for k in range(16):
                            P.mm(bg.ap, wg.ap[:, k, sub * 128:(sub + 1) * 128], hb.ap[:, k, :], k == 0, k == 15,
                                 [wg.r, hb.r], [bg.r])
                        for k in range(16):
                            P.mm(bu.ap, wu.ap[:, k, sub * 128:(sub + 1) * 128], hb.ap[:, k, :], k == 0, k == 15,
                                 [wu.r, hb.r], [bu.r])
                        sg = sgr.next()
                        P.act(sg.ap, bg.ap, AF.Silu, [bg.r], [sg.r])
                        P.tt(actT.ap[:, f, :], sg.ap, bu.ap, ALU.mult, [sg.r, bu.r], [actT.r])
                for cbk in range(16):
                    wd = wd_r.next()
                    if tb == 0:
                        P.dma(wd.ap, wdv[:, :, cbk * 128:(cbk + 1) * 128], writes=[wd.r], q="pool", key="ldw")
                        P.dma(wd16[cbk], wd.ap.rearrange("p k c -> p (k c)"), reads=[wd.r], q="sp", key="st")
                    else:
                        P.dma(wd.ap.rearrange("p k c -> p (k c)"), wd16[cbk], writes=[wd.r], q="pool", key="ldw")
                    bk = banks.next()
                    for f in range(44):
                        P.mm(bk.ap, wd.ap[:, f, :], actT.ap[:, f, :], f == 0, f == 43, [wd.r, actT.r], [bk.r])
                    P.affine(P.ev(), f2.ap[:, cbk, :], bk.ap, mod.ap[:, 80 + cbk, cnd_i:cnd_i + 1], 0.0,
                             [bk.r, mod.r], [f2.r])
                for t4 in range(4):
                    tt = tb * 4 + t4
                    ts_ = slice(tt * 128, (tt + 1) * 128)
                    x1 = x1r.next()
                    P.dma(x1.ap, x1s[ts_, :], writes=[x1.r])
                    y = yr.next()
                    for q4 in range(4):
                        bk = banks.next()
                        for i in range(4):
                            k = q4 * 4 + i
                            P.tr(bk.ap[:, i * 128:(i + 1) * 128], f2.ap[:, k, t4 * 128:(t4 + 1) * 128], ident,
                                 [f2.r, cst.r], [bk.r], inc=(i == 3))
                        P.stt(y.ap[:, q4 * 512:(q4 + 1) * 512], x1.ap[:, q4 * 512:(q4 + 1) * 512], ALPHA, bk.ap,
                              ALU.mult, ALU.add, [x1.r, bk.r], [y.r])
                    s_ = stt_.next()
                    layer_norm_tile(y, s_, lg, lb, y.ap, [], y.r)
                    P.dma(y_all[ts_, :], y.ap, reads=[y.r], q="act", key="st")
                if tb == 0:
                    P.barrier()
            P.barrier()
            sb.off = mark

        P.barrier(streams=["sp"])
        P.emit()
    return nc


VEC_SPECS = [("mu", 26), ("qg", 4), ("kvg", 2), ("w0f", 8), ("w0b", 8), ("a0f", 8), ("a0b", 8),
             ("k_k", 8), ("k_a", 8), ("r_k", 8), ("gn_g", 8), ("gn_b", 8)]
VO = {}
_o = 0
for _n, _c in VEC_SPECS:
    VO[_n] = _o
    _o += _c
NVEC = (_o + 7) // 8 * 8
CONST_SPECS = [("ident", 128), ("ones", 128), ("blk", 128), ("m_su", 128), ("m_iu", 128), ("m_sl", 128), ("m_il", 128)]
CO = {}
_o = 0
for _n, _c in CONST_SPECS:
    CO[_n] = _o
    _o += _c
NCONST = _o
NCB = 256 + 4 * 128 + 2 * 384


def _colmajor(v):
    v = np.asarray(v, np.float32).reshape(-1)
    return np.ascontiguousarray(v.reshape(-1, 128).T)


def make_consts():
    c = np.zeros((128, NCONST), np.float32)
    i = np.arange(128)
    c[:, CO["ident"]:CO["ident"] + 128] = np.eye(128, dtype=np.float32)
    c[:, CO["ones"]:CO["ones"] + 128] = 1.0
    c[:, CO["blk"]:CO["blk"] + 128] = ((i[:, None] // 64) == (i[None, :] // 64)).astype(np.float32) / 64.0
    s, t = i[:, None], i[None, :]
    c[:, CO["m_su"]:CO["m_su"] + 128] = (s < t)
    c[:, CO["m_iu"]:CO["m_iu"] + 128] = (s <= t)
    c[:, CO["m_sl"]:CO["m_sl"] + 128] = (s > t)
    c[:, CO["m_il"]:CO["m_il"] + 128] = (s >= t)
    return c


def make_rope():
    n = NS
    rows = n // 64
    row = np.repeat(np.arange(rows), 64).astype(np.float32)
    col = np.tile(np.arange(64), rows).astype(np.float32)
    freqs = (np.float32(10000.0) ** (-np.arange(16, dtype=np.float32) / np.float32(16))).astype(np.float32)
    ang = np.concatenate([row[:, None] * freqs, col[:, None] * freqs], -1).astype(np.float32)
    cos, sin = np.cos(ang).astype(np.float32), np.sin(ang).astype(np.float32)
    t = np.zeros((64, 2, n), np.float32)
    t[0:32, 0] = cos.T
    t[32:64, 0] = cos.T
    t[0:32, 1] = -sin.T
    t[32:64, 1] = sin.T
    return t


def make_core_inputs(inp, core):
    f = lambda a: np.ascontiguousarray(np.asarray(a, np.float32))
    b = core
    x_all = np.concatenate([inp["x_sample"][b], inp["x_prompt"][2 * b], inp["x_prompt"][2 * b + 1]], 0)
    cond = np.stack([inp["c"][b], inp["c_ctx"]], 0)
    condT = np.ascontiguousarray(cond.reshape(2, 16, 128).transpose(2, 1, 0))
    vec = np.zeros((128, NVEC), np.float32)
    src = {"mu": inp["tok_shift_mu"][0], "qg": inp["q_norm_g"][0], "kvg": inp["kv_norm_g"][0],
           "w0f": inp["w0_fwd"][0], "w0b": inp["w0_bwd"][0], "a0f": inp["a0_fwd"][0], "a0b": inp["a0_bwd"][0],
           "k_k": inp["k_k"][0], "k_a": inp["k_a"][0], "r_k": inp["r_k"][0], "gn_g": inp["gn_g"][0],
           "gn_b": inp["gn_b"][0]}
    for n_, c_ in VEC_SPECS:
        vec[:, VO[n_]:VO[n_] + c_] = _colmajor(src[n_])
    return {
        "x_all": f(x_all), "cache_ckv": f(inp["cache_ckv"][b, 0]), "cache_kr": f(inp["cache_krope"][b, 0]),
        "st_f": f(inp["state_wkv_fwd"][b, 0]), "st_b": f(inp["state_wkv_bwd"][b, 0]),
        "condT": f(condT), "bmodT": _colmajor(inp["b_mod"][0]), "vecs": vec,
        "w_mod": f(inp["w_mod"][0]), "w_in": f(inp["w_in"][0]), "w_uq": f(inp["w_uq"][0]),
        "w_uk": f(inp["w_uk"][0]), "w_uv": f(inp["w_uv"][0]),
        "w_up_f": f(inp["w_up_fwd"][0]), "w_up_b": f(inp["w_up_bwd"][0]),
        "a_up_f": f(inp["a_up_fwd"][0]), "a_up_b": f(inp["a_up_bwd"][0]), "g_up": f(inp["g_up"][0]),
        "w_out": f(inp["w_out"][0]),
        "lnv": f(np.stack([inp["ln1_g"][0], inp["ln1_b"][0], inp["ln2_g"][0], inp["ln2_b"][0]], 0)),
        "w_gate": f(inp["w_ffn_gate"][0]), "w_upf": f(inp["w_ffn_up"][0]), "w_down": f(inp["w_ffn_down"][0]),
    }


def kernel(**inputs):
    inp = {k: np.asarray(v) for k, v in inputs.items()}
    nc = build_program()
    consts = make_consts()
    rope = make_rope()
    in_maps = []
    for core in range(8):
        m = make_core_inputs(inp, core)
        m["consts"] = consts
        m["ropeT"] = rope
        in_maps.append(m)
    res = run_bass_kernel_spmd(nc, in_maps, core_ids=list(range(8)))
    R = res.results
    y_prompt = np.zeros((16, 256, D), np.float32)
    y_sample = np.zeros((8, NS, D), np.float32)
    new_ckv = np.zeros((16, 1, 256, 256), np.float32)
    new_kr = np.zeros((16, 1, 256, 64), np.float32)
    new_sf = np.zeros((16, 1, 16, 64, 64), np.float32)
    new_sb = np.zeros((16, 1, 16, 64, 64), np.float32)
    for b in range(8):
        r = R[b]
        y_sample[b] = r["y_all"][0:NS]
        y_prompt[2 * b] = r["y_all"][NS:NS + 256]
        y_prompt[2 * b + 1] = r["y_all"][NS + 256:NS + 512]
        new_ckv[2 * b, 0] = r["nckv"][0:256]
        new_ckv[2 * b + 1, 0] = r["nckv"][256:512]
        new_kr[2 * b, 0] = r["nkr"][0:256]
        new_kr[2 * b + 1, 0] = r["nkr"][256:512]
        new_sf[2 * b, 0] = r["nsf"][0]
        new_sf[2 * b + 1, 0] = r["nsf"][1]
        new_sb[2 * b, 0] = r["nsb"][0]
        new_sb[2 * b + 1, 0] = r["nsb"][1]
    return (y_prompt, y_sample, new_ckv, new_kr, new_sf, new_sb)
```

```python
import math
from contextlib import ExitStack
import numpy as np
import concourse.bass as bass
import concourse.mybir as mybir
from concourse.bass_utils import run_bass_kernel_spmd

F32 = mybir.dt.float32
BF16 = mybir.dt.bfloat16
F32R = mybir.dt.float32r
AF = mybir.ActivationFunctionType
ALU = mybir.AluOpType

D = 2048
NS = 2048
NP_ = 256
G = NS + 2 * NP_
IN_COLS = 4160
PROJ_ROWS = 4224
OFF_KV, OFF_KR, OFF_RW = 512, 768, 832
DFF = 5632
ALPHA = 2.0 ** 0.25
LN_EPS = 1e-5
RMS_EPS = 1e-6
GN_EPS = 64e-5
SCALE = 1.0 / math.sqrt(192.0)
NEG_EXP_HALF = -math.exp(-0.5)
SEM_LIMIT = 30000
STRICT_SAME = "waw"
DSTOP = 99
POOL_ENG = "dve"
TDT = F32R
TSW = 4
PREC = {"lora": BF16, "scan": BF16, "T": F32R, "Tt": BF16}
C = 128


class Res:
    __slots__ = ("lw", "rd", "excl")

    def __init__(self, excl=False):
        self.lw = None
        self.rd = []
        self.excl = excl


class Tl:
    __slots__ = ("ap", "r")

    def __init__(self, ap, r=None):
        self.ap = ap
        self.r = r if r is not None else Res()


class Prog:
    STREAMS = ("pe", "act", "dve", "pool", "sp")

    def __init__(self, nc, stack):
        self.nc = nc
        self.stack = stack
        self.q = {s: [] for s in self.STREAMS}
        self.chains = {}
        self.waited = {s: {} for s in self.STREAMS}
        self.ev_i = 0
        self.dma_slots = {}

    def _chain(self, key):
        c = self.chains.get(key)
        if c is None:
            h = self.stack.enter_context(self.nc.semaphore(f"s_{key}_0"))
            c = dict(sems=[h], cur=0, cnt=0)
            self.chains[key] = c
        return c

    def _bump(self, key, step):
        c = self._chain(key)
        if c["cnt"] + step > SEM_LIMIT:
            h = self.stack.enter_context(self.nc.semaphore(f"s_{key}_{len(c['sems'])}"))
            c["sems"].append(h)
            c["cur"] += 1
            c["cnt"] = 0
        c["cnt"] += step
        return (key, c["cur"]), c["cnt"]

    def _peek(self, key, step):
        c = self._chain(key)
        if c["cnt"] + step > SEM_LIMIT:
            return (key, c["cur"] + 1), step
        return (key, c["cur"]), c["cnt"] + step

    def op(self, stream, fn, reads=(), writes=(), inc=True, semkey=None, step=1):
        if semkey is None:
            semkey = stream
        need = {}

        def add(dep, mode):
            if dep is None:
                return
            sk, val, st = dep
            if st == stream and sk[0] == stream:
                if mode == "rr":
                    return
                if mode != "raw" and (stream == "pe" or not STRICT_SAME):
                    return
                if mode == "war" and STRICT_SAME == "waw":
                    return
            if need.get(sk, 0) < val:
                need[sk] = val

        for r in reads:
            add(r.lw, "raw")
            if r.excl:
                for d in r.rd:
                    add(d, "rr")
        for w in writes:
            add(w.lw, "waw")
            for d in w.rd:
                add(d, "war")
        wt = self.waited[stream]
        waits = []
        for sk, val in need.items():
            newer = sk[0] in self.STREAMS and any(k2[0] == sk[0] and k2[1] > sk[1] for k2 in wt)
            if newer or wt.get(sk, 0) >= val:
                continue
            wt[sk] = val
            waits.append((sk, val))
        if inc:
            sk, val = self._bump(semkey, step)
        else:
            sk, val = self._peek(semkey, step)
        me = (sk, val, stream)
        for r in reads:
            r.rd.append(me)
        for w in writes:
            w.lw = me
            w.rd = []
        self.q[stream].append((waits, fn, (sk if inc else None), step))

    def sem_handle(self, sk):
        return self.chains[sk[0]]["sems"][sk[1]]

    def barrier(self, streams=None):
        if streams is None:
            self.dma_slots = {}
        for s in (streams or self.STREAMS):
            waits = []
            wt = self.waited[s]
            for key, c in self.chains.items():
                if c["cnt"] == 0:
                    continue
                sk = (key, c["cur"])
                if wt.get(sk, 0) >= c["cnt"]:
                    continue
                wt[sk] = c["cnt"]
                waits.append((sk, c["cnt"]))
            if waits:
                self.q[s].append((waits, None, None, 0))

    def emit(self):
        nc = self.nc
        prog = self
        with nc.Block() as block:
            def run(stream):
                def body(eng):
                    for waits, fn, sk, step in prog.q[stream]:
                        for wsk, val in waits:
                            eng.wait_ge(prog.sem_handle(wsk), val)
                        if fn is None:
                            continue
                        ins = fn(eng)
                        if sk is not None:
                            ins.then_inc(prog.sem_handle(sk), step)
                return body
            block.tensor(run("pe"))
            block.scalar(run("act"))
            block.vector(run("dve"))
            block.gpsimd(run("pool"))
            block.sync(run("sp"))

    def ev(self):
        self.ev_i += 1
        return "act" if self.ev_i % 2 else "dve"

    def dma(self, out, in_, reads=(), writes=(), q="sp", key="ld"):
        res = writes[0] if len(writes) else reads[0]
        fam = "w" if q == "pool" else "h"
        slot = self.dma_slots.get((fam, id(res)))
        if slot is None:
            slot = sum(1 for k_ in self.dma_slots if k_[0] == fam)
            self.dma_slots[(fam, id(res))] = slot
        self.op(q, lambda e: e.dma_start(out=out, in_=in_), reads, writes, semkey=f"dma{fam}{slot}", step=16)

    def mm(self, out, lhsT, rhs, start, stop, reads, writes, inc=None):
        if inc is None:
            inc = stop
        self.op("pe", lambda e: e.matmul(out, lhsT, rhs, start=start, stop=stop), reads, writes, inc=inc)

    def tr(self, out, in_, ident, reads, writes, inc=True):
        self.op("pe", lambda e: e.transpose(out, in_, ident), reads, writes, inc=inc)

    def copy(self, eng, out, in_, reads, writes):
        if eng == "act":
            self.op("act", lambda e: e.copy(out, in_), reads, writes)
        else:
            self.op(eng, lambda e: e.tensor_copy(out, in_), reads, writes)

    def act(self, out, in_, func, reads, writes, scale=1.0, bias=0.0, accum=None):
        kw = {}
        if not (isinstance(bias, float) and bias == 0.0):
            kw["bias"] = bias
        if not (isinstance(scale, float) and scale == 1.0):
            kw["scale"] = scale
        if accum is not None:
            kw["accum_out"] = accum
        self.op("act", lambda e: e.activation(out=out, in_=in_, func=func, **kw), reads, writes)

    def affine(self, eng, out, in_, scale, bias, reads, writes):
        nobias = isinstance(bias, float) and bias == 0.0
        if eng == "act":
            self.act(out, in_, AF.Identity, reads, writes, scale=scale, bias=bias)
        elif nobias:
            self.op("dve", lambda e: e.tensor_scalar(out, in_, scale, None, op0=ALU.mult), reads, writes)
        else:
            self.op("dve", lambda e: e.tensor_scalar(out, in_, scale, bias, op0=ALU.mult, op1=ALU.add), reads, writes)

    def call(self, stream, name, args, reads, writes, **kw):
        self.op(stream, lambda e: getattr(e, name)(*args, **kw), reads, writes)

    def tt(self, out, a, b, op, reads, writes, eng="dve"):
        self.op(eng, lambda e: e.tensor_tensor(out, a, b, op), reads, writes)

    def ts(self, out, a, s1, s2, op0, op1, reads, writes, eng="dve"):
        if s2 is None:
            self.op(eng, lambda e: e.tensor_scalar(out, a, s1, None, op0=op0), reads, writes)
        else:
            self.op(eng, lambda e: e.tensor_scalar(out, a, s1, s2, op0=op0, op1=op1), reads, writes)

    def stt(self, out, in0, scalar, in1, op0, op1, reads, writes, eng="dve"):
        self.op(eng, lambda e: e.scalar_tensor_tensor(out, in0, scalar, in1, op0, op1), reads, writes)


class SBAlloc:
    def __init__(self, big, nwords):
        self.big = big
        self.n = nwords
        self.off = 0

    def a(self, free_shape, dt=F32, parts=(0, 128)):
        n = int(np.prod(free_shape))
        words = n if dt in (F32, F32R) else (n + 1) // 2
        words = (words + 7) // 8 * 8
        assert self.off + words <= self.n, f"SBUF overflow {self.off}+{words}>{self.n}"
        ap = self.big[parts[0]:parts[1], self.off:self.off + words]
        self.off += words
        if dt != F32:
            ap = ap.bitcast(dt)
        ap = ap[:, 0:n]
        if len(free_shape) == 2:
            ap = ap.rearrange("p (a b) -> p a b", a=free_shape[0])
        elif len(free_shape) == 3:
            ap = ap.rearrange("p (a b c) -> p a b c", a=free_shape[0], b=free_shape[1])
        elif len(free_shape) == 4:
            ap = ap.rearrange("p (a b c d) -> p a b c d", a=free_shape[0], b=free_shape[1], c=free_shape[2])
        return ap

    def t(self, free_shape, dt=F32, parts=(0, 128)):
        return Tl(self.a(free_shape, dt, parts))

    def ring(self, n, free_shape, dt=F32):
        return [self.t(free_shape, dt) for _ in range(n)]


class Ring:
    def __init__(self, items):
        self.items = items
        self.i = -1

    def next(self):
        self.i = (self.i + 1) % len(self.items)
        return self.items[self.i]

    def take(self):
        it = self.next()
        self.items.remove(it)
        self.i -= 1
        return it

    def give(self, it):
        self.items.append(it)


SB_WORDS = 44 * 1024
SBR_WORDS = 12 * 512


def build_program(debug=False, stages="ABCDEF", dbg_seqs=None, dbg_pairs=None):
    nc = bass.Bass("TRN2", target_bir_lowering=False)
    dt_in = lambda name, shape: nc.dram_tensor(name, list(shape), F32, kind="ExternalInput").ap()
    dt_out = lambda name, shape: nc.dram_tensor(name, list(shape), F32, kind="ExternalOutput").ap()
    scr_kind = "ExternalOutput" if debug else "Internal"
    x_all = dt_in("x_all", [G, D])
    cache_ckv = dt_in("cache_ckv", [256, 256])
    cache_kr = dt_in("cache_kr", [256, 64])
    st_f = dt_in("st_f", [16, 64, 64])
    st_b = dt_in("st_b", [16, 64, 64])
    condT = dt_in("condT", [128, 16, 2])
    bmodT = dt_in("bmodT", [128, 96])
    vecs = dt_in("vecs", [128, NVEC])
    consts = dt_in("consts", [128, NCONST])
    ropeT = dt_in("ropeT", [64, 2, NS])
    w_mod = dt_in("w_mod", [D, 6 * D])
    w_in = dt_in("w_in", [D, IN_COLS])
    w_uq = dt_in("w_uq", [512, 1536])
    w_uk = dt_in("w_uk", [256, 1024])
    w_uv = dt_in("w_uv", [256, 1024])
    w_up_f = dt_in("w_up_f", [64, 1024])
    w_up_b = dt_in("w_up_b", [64, 1024])
    a_up_f = dt_in("a_up_f", [64, 1024])
    a_up_b = dt_in("a_up_b", [64, 1024])
    g_up = dt_in("g_up", [128, 1024])
    w_out = dt_in("w_out", [D, D])
    lnv = dt_in("lnv", [4, D])
    w_gate = dt_in("w_gate", [D, DFF])
    w_upf = dt_in("w_upf", [D, DFF])
    w_down = dt_in("w_down", [DFF, D])
    y_all = dt_out("y_all", [G, D])
    nckv = dt_out("nckv", [2 * NP_, 256])
    nkr = dt_out("nkr", [2 * NP_, 64])
    nsf = dt_out("nsf", [2, 16, 64, 64])
    nsb = dt_out("nsb", [2, 16, 64, 64])
    projT = nc.dram_tensor("projT", [PROJ_ROWS, G], F32, kind=scr_kind).ap()
    mixT = nc.dram_tensor("mixT", [D, G], BF16, kind=scr_kind).ap()
    x1s = nc.dram_tensor("x1s", [G, D], F32, kind=scr_kind).ap()
    hT = nc.dram_tensor("hT", [D, G], BF16, kind="Internal").ap()
    wg16 = nc.dram_tensor("wg16", [22, 128, 16 * 256], BF16, kind="Internal").ap()
    wu16 = nc.dram_tensor("wu16", [22, 128, 16 * 256], BF16, kind="Internal").ap()
    wd16 = nc.dram_tensor("wd16", [16, 128, 44 * 128], BF16, kind="Internal").ap()
    if debug:
        dbg_mod = dt_out("dbg_mod", [128, 96, 2])
        dbg_mix = dt_out("dbg_mix", [D, G])

    with ExitStack() as st:
        P = Prog(nc, st)
        big = st.enter_context(nc.sbuf_tensor("big", [128, SB_WORDS], F32))
        psum = st.enter_context(nc.psum_tensor("psum", [128, 8, 512], F32))
        sb = SBAlloc(big, SB_WORDS)
        bigr = st.enter_context(nc.sbuf_tensor("bigr", [128, SBR_WORDS], TDT))
        rtiles = [Tl(bigr[:, i * 512:(i + 1) * 512]) for i in range(12)]
        banks = Ring([Tl(psum[:, i, :], Res(excl=True)) for i in range(8)])

        cst = sb.t([NCONST])
        P.dma(cst.ap, consts, writes=[cst.r])
        cf = lambda name, n: cst.ap[:, CO[name]:CO[name] + n]
        ident = cf("ident", 128)
        ones_f = cf("ones", 128)
        blk_f = cf("blk", 128)
        vec = sb.t([NVEC])
        P.dma(vec.ap, vecs, writes=[vec.r])
        vcol = lambda name, j=0: vec.ap[:, VO[name] + j:VO[name] + j + 1]
        cb = sb.t([NCB], BF16)
        P.copy("dve", cb.ap[:, 0:128], ident, [cst.r], [cb.r])
        P.copy("dve", cb.ap[:, 128:256], ones_f, [cst.r], [cb.r])
        for i, nm in enumerate(("m_su", "m_iu", "m_sl", "m_il")):
            P.copy("dve", cb.ap[:, 256 + 128 * i:384 + 128 * i], cf(nm, 128), [cst.r], [cb.r])
        for j, (ms_, mi_) in enumerate((("m_su", "m_iu"), ("m_sl", "m_il"))):
            o_ = 768 + 384 * j
            P.copy("dve", cb.ap[:, o_:o_ + 128], cf(ms_, 128), [cst.r], [cb.r])
            P.copy("dve", cb.ap[:, o_ + 128:o_ + 256], cf(mi_, 128), [cst.r], [cb.r])
            P.copy("dve", cb.ap[:, o_ + 256:o_ + 384], cf(mi_, 128), [cst.r], [cb.r])
        mask3 = [cb.ap[:, 768:1152], cb.ap[:, 1152:1536]]
        ident_b = cb.ap[:, 0:128]
        ones_b = cb.ap[:, 128:256]
        mask_b = {nm: cb.ap[:, 256 + 128 * i:384 + 128 * i] for i, nm in enumerate(("su", "iu", "sl", "il"))}
        dv = sb.t([64])
        P.ts(dv.ap[:, 0:26], vec.ap[:, VO["mu"]:VO["mu"] + 26], -1.0, 1.0, ALU.mult, ALU.add, [vec.r], [dv.r])
        P.ts(dv.ap[:, 26:52], vec.ap[:, VO["mu"]:VO["mu"] + 26], 0.5, None, ALU.mult, None, [vec.r], [dv.r])
        P.ts(dv.ap[:, 52:60], vec.ap[:, VO["k_a"]:VO["k_a"] + 8], -1.0, 1.0, ALU.mult, ALU.add, [vec.r], [dv.r])
        omm = lambda i: dv.ap[:, i:i + 1]
        hmu = lambda i: dv.ap[:, 26 + i:27 + i]
        omka = lambda i: dv.ap[:, 52 + i:53 + i]
        epsc = sb.t([8])
        P.call("dve", "memset", (epsc.ap[:, 0:1], RMS_EPS), [], [epsc.r])
        P.call("dve", "memset", (epsc.ap[:, 1:2], GN_EPS), [], [epsc.r])
        eps_rms = epsc.ap[:, 0:1]
        eps_gn = epsc.ap[:, 1:2]
        mod = sb.t([96, 2])
        mod1p = sb.t([32, 2])
        base_mark = sb.off

        if "A" in stages:
            cnd = sb.t([16, 2])
            bm = sb.t([96])
            P.dma(cnd.ap, condT, writes=[cnd.r])
            P.dma(bm.ap, bmodT, writes=[bm.r])
            scn = sb.t([16, 2])
            P.act(scn.ap, cnd.ap, AF.Silu, [cnd.r], [scn.r])
            wring = Ring(sb.ring(2, [16, 512]))
            wv = w_mod.rearrange("(k p) c -> p k c", p=128)
            mrow = sb.t([6 * D])
            for blk in range(24):
                wt = wring.next()
                P.dma(wt.ap, wv[:, :, blk * 512:(blk + 1) * 512], writes=[wt.r])
                bk = banks.next()
                for k in range(16):
                    P.mm(bk.ap[0:2, :], scn.ap[:, k, :], wt.ap[:, k, :], k == 0, k == 15, [wt.r, scn.r], [bk.r])
                P.copy(P.ev(), mrow.ap[0:2, blk * 512:(blk + 1) * 512], bk.ap[0:2, :], [bk.r], [mrow.r])
            bk = banks.next()
            for j in range(96):
                P.tr(bk.ap[:, 2 * j:2 * j + 2], mrow.ap[0:2, j * 128:(j + 1) * 128], ident[0:2, 0:2],
                     [mrow.r, cst.r], [bk.r], inc=(j == 95))
            P.tt(mod.ap, bk.ap[:, 0:192].rearrange("p (j c) -> p j c", c=2),
                 bm.ap.unsqueeze(2).to_broadcast([128, 96, 2]), ALU.add, [bk.r, bm.r], [mod.r])
            P.ts(mod1p.ap[:, 0:16, :], mod.ap[:, 16:32, :], 1.0, None, ALU.add, None, [mod.r], [mod1p.r])
            P.ts(mod1p.ap[:, 16:32, :], mod.ap[:, 64:80, :], 1.0, None, ALU.add, None, [mod.r], [mod1p.r])
            if debug:
                P.dma(dbg_mod, mod.ap, reads=[mod.r], q="act", key="st")
            P.barrier()
            sb.off = base_mark

        cond_of_tile = lambda tt: 0 if tt < NS // 128 else 1

        if "B" in stages:
            xmT = sb.a([16, G], BF16)
            xm_r = [Res() for _ in range(G // 512)]
            xring = Ring(sb.ring(2, [D]))
            for tt in range(G // 128):
                xt = xring.next()
                P.dma(xt.ap, x_all[tt * 128:(tt + 1) * 128, :], writes=[xt.r])
                cnd_i = cond_of_tile(tt)
                for q4 in range(4):
                    bk = banks.next()
                    for i in range(4):
                        k = q4 * 4 + i
                        P.tr(bk.ap[:, i * 128:(i + 1) * 128], xt.ap[:, k * 128:(k + 1) * 128], ident,
                             [xt.r, cst.r], [bk.r], inc=(i == 3))
                    eng_ = P.ev()
                    for i in range(4):
                        k = q4 * 4 + i
                        P.affine(eng_, xmT[:, k, tt * 128:(tt + 1) * 128], bk.ap[:, i * 128:(i + 1) * 128],
                                 mod1p.ap[:, k, cnd_i:cnd_i + 1], mod.ap[:, k, cnd_i:cnd_i + 1],
                                 [bk.r, mod.r, mod1p.r], [xm_r[tt // 4]])
            wring = Ring(sb.ring(3, [16, 256], BF16))
            oring = Ring(sb.ring(4, [512]))
            wv = w_in.rearrange("(k p) c -> p k c", p=128)
            for wb in range(17):
                wt = wring.next()
                if wb < 16:
                    P.dma(wt.ap, wv[:, :, wb * 256:(wb + 1) * 256], writes=[wt.r], q="pool", key="ldw")
                    nsub = 2
                else:
                    P.dma(wt.ap[:, :, 0:64], wv[:, :, 4096:4160], writes=[wt.r], q="pool", key="ldw")
                    P.dma(wt.ap[:, :, 64:96], wv[:, :, 800:832], writes=[wt.r], q="pool", key="ldw")
                    P.dma(wt.ap[:, :, 96:128], wv[:, :, 768:800], writes=[wt.r], q="pool", key="ldw")
                    nsub = 1
                for sub in range(nsub):
                    row0 = wb * 256 + sub * 128
                    for tb in range(G // 512):
                        bk = banks.next()
                        for k in range(16):
                            P.mm(bk.ap, wt.ap[:, k, sub * 128:(sub + 1) * 128], xmT[:, k, tb * 512:(tb + 1) * 512],
                                 k == 0, k == 15, [wt.r, xm_r[tb]], [bk.r])
                        ot = oring.next()
                        P.copy(P.ev(), ot.ap, bk.ap, [bk.r], [ot.r])
                        P.dma(projT[row0:row0 + 128, tb * 512:(tb + 1) * 512], ot.ap, reads=[ot.r], q="act", key="st")
            P.barrier()
            sb.off = base_mark

        seqs = [(0, NS, True, 0), (NS, NP_, False, 0), (NS + NP_, NP_, False, 1)]
        if dbg_seqs is not None:
            seqs = [seqs[i] for i in dbg_seqs]
        pair_list = list(range(8)) if dbg_pairs is None else list(dbg_pairs)

        if "C" in stages:
            cgroups = [(0, NS, 1, True, 0), (NS, NP_, 2, False, 0)]
            if dbg_seqs is not None:
                cgroups = [cgroups[i] for i in sorted(set(min(i, 1) for i in dbg_seqs))]
            for (t0, Sq, nseq, samp, pi) in cgroups:
                mark = sb.off
                S = Sq * nseq
                NK = S + (256 if samp else 0)
                nkt = NK // 128
                ablocks = []
                for q_ in range(nseq):
                    kts = list(range(q_ * Sq // 128, (q_ + 1) * Sq // 128)) + list(range(S // 128, nkt))
                    for o_ in range(q_ * Sq, (q_ + 1) * Sq, 512):
                        ablocks.append((o_, min(512, (q_ + 1) * Sq - o_), kts))
                tbs = [(o, min(512, S - o)) for o in range(0, S, 512)]
                kbs = [(o, min(512, NK - o)) for o in range(0, NK, 512)]
                stg = Ring(sb.ring(2, [S]))
                ostg = Ring(sb.ring(2, [320]))
                qdg = sb.t([4, S], BF16)
                sq = sb.t([4, S], BF16)
                rq = sb.t([S])
                rkv = sb.t([S])
                ckv_all = sb.t([2, NK], BF16)
                kr_all = sb.t([NK], BF16)
                if samp:
                    rope = sb.t([2, NS])
                    P.dma(rope.ap[0:64], ropeT, writes=[rope.r])
                for k in range(4):
                    s_ = stg.next()
                    P.dma(s_.ap, projT[k * 128:(k + 1) * 128, t0:t0 + S], writes=[s_.r])
                    P.act(sq.ap[:, k, :], s_.ap, AF.Square, [s_.r], [sq.r])
                    P.ts(qdg.ap[:, k, :], s_.ap, vcol("qg", k), None, ALU.mult, None, [s_.r, vec.r], [qdg.r])
                for (o, n) in tbs:
                    bk = banks.next()
                    for k in range(4):
                        P.mm(bk.ap[:, 0:n], ones_b, sq.ap[:, k, o:o + n], k == 0, k == 3, [cb.r, sq.r], [bk.r])
                    P.act(rq.ap[:, o:o + n], bk.ap[:, 0:n], AF.Sqrt, [bk.r, epsc.r], [rq.r], scale=1.0 / 512, bias=eps_rms)
                P.call("dve", "reciprocal", (rq.ap, rq.ap), [rq.r], [rq.r])
                ckf = sb.t([2, S])
                for k in range(2):
                    s_ = stg.next()
                    P.dma(s_.ap, projT[OFF_KV + k * 128:OFF_KV + (k + 1) * 128, t0:t0 + S], writes=[s_.r])
                    P.act(sq.ap[:, k, :], s_.ap, AF.Square, [s_.r], [sq.r])
                    P.ts(ckf.ap[:, k, :], s_.ap, vcol("kvg", k), None, ALU.mult, None, [s_.r, vec.r], [ckf.r])
                for (o, n) in tbs:
                    bk = banks.next()
                    for k in range(2):
                        P.mm(bk.ap[:, 0:n], ones_b, sq.ap[:, k, o:o + n], k == 0, k == 1, [cb.r, sq.r], [bk.r])
                    P.act(rkv.ap[:, o:o + n], bk.ap[:, 0:n], AF.Sqrt, [bk.r, epsc.r], [rkv.r], scale=1.0 / 256, bias=eps_rms)
                P.call("dve", "reciprocal", (rkv.ap, rkv.ap), [rkv.r], [rkv.r])
                for k in range(2):
                    P.tt(ckf.ap[:, k, :], ckf.ap[:, k, :], rkv.ap, ALU.mult, [ckf.r, rkv.r], [ckf.r])
                    P.copy("act", ckv_all.ap[:, k, 0:S], ckf.ap[:, k, :], [ckf.r], [ckv_all.r])
                krf = sb.t([S], parts=(0, 64))
                P.dma(krf.ap, projT[OFF_KR:OFF_KR + 64, t0:t0 + S], writes=[krf.r])
                if samp:
                    krs = sb.t([S], parts=(0, 64))
                    P.dma(krs.ap, projT[4160:4224, t0:t0 + S], writes=[krs.r])
                    P.tt(krf.ap, krf.ap, rope.ap[0:64, 0, :], ALU.mult, [krf.r, rope.r], [krf.r])
                    P.tt(krs.ap, krs.ap, rope.ap[0:64, 1, :], ALU.mult, [krs.r, rope.r], [krs.r])
                    P.tt(kr_all.ap[0:64, 0:S], krf.ap, krs.ap, ALU.add, [krf.r, krs.r], [kr_all.r])
                    cc = sb.t([2, 256])
                    P.dma(cc.ap, cache_ckv.rearrange("(t p) f -> p t f", p=128), writes=[cc.r])
                    ck = sb.t([2, 64])
                    P.dma(ck.ap, cache_kr.rearrange("(t p) f -> p t f", p=128), writes=[ck.r])
                    for tt in range(2):
                        bk = banks.next()
                        for k in range(2):
                            P.tr(bk.ap[:, k * 128:(k + 1) * 128], cc.ap[:, tt, k * 128:(k + 1) * 128], ident,
                                 [cc.r, cst.r], [bk.r], inc=False)
                        P.tr(bk.ap[0:64, 256:384], ck.ap[:, tt, :], ident, [ck.r, cst.r], [bk.r])
                        eng_ = P.ev()
                        for k in range(2):
                            P.copy(eng_, ckv_all.ap[:, k, S + tt * 128:S + (tt + 1) * 128],
                                   bk.ap[:, k * 128:(k + 1) * 128], [bk.r], [ckv_all.r])
                        P.copy(eng_, kr_all.ap[0:64, S + tt * 128:S + (tt + 1) * 128], bk.ap[0:64, 256:384],
                               [bk.r], [kr_all.r])
                else:
                    P.copy("act", kr_all.ap[0:64, 0:S], krf.ap, [krf.r], [kr_all.r])
                    for tt in range(S // 128):
                        bk = banks.next()
                        for k in range(2):
                            P.tr(bk.ap[:, k * 128:(k + 1) * 128], ckf.ap[:, k, tt * 128:(tt + 1) * 128], ident,
                                 [ckf.r, cst.r], [bk.r], inc=False)
                        P.tr(bk.ap[:, 256:320], krf.ap[0:64, tt * 128:(tt + 1) * 128], ident[0:64, 0:64],
                             [krf.r, cst.r], [bk.r])
                        o_ = ostg.next()
                        P.copy(P.ev(), o_.ap[:, 0:320], bk.ap[:, 0:320], [bk.r], [o_.r])
                        r0 = pi * NP_ + tt * 128
                        P.dma(nckv[r0:r0 + 128, :], o_.ap[:, 0:256], reads=[o_.r], q="act", key="st")
                        P.dma(nkr[r0:r0 + 128, :], o_.ap[:, 256:320], reads=[o_.r], q="act", key="st")
                wq_ring = Ring(sb.ring(2, [4, 256], BF16))
                wk_ring = Ring(sb.ring(2, [2, 256], BF16))
                qn = sb.t([S], BF16)
                qr = sb.t([S], BF16)
                kn = sb.t([NK], BF16)
                vh = sb.t([nkt, 128], BF16)
                tmpr = Ring(sb.ring(2, [2, 512]))
                pring = Ring(sb.ring(2, [512], BF16))
                oring = Ring(sb.ring(2, [512], BF16))
                rring = Ring(sb.ring(1, [512]))
                uqv = w_uq.rearrange("(k p) c -> p k c", p=128)
                ukv = w_uk.rearrange("(k p) c -> p k c", p=128)
                uvv = w_uv.rearrange("(k p) c -> p k c", p=128)
                for h in range(8):
                    wq = wq_ring.next()
                    c0 = h * 192
                    P.dma(wq.ap[:, :, 0:192], uqv[:, :, c0:c0 + 192], writes=[wq.r], q="pool", key="ldw")
                    P.dma(wq.ap[:, :, 192:224], uqv[:, :, c0 + 160:c0 + 192], writes=[wq.r], q="pool", key="ldw")
                    P.dma(wq.ap[:, :, 224:256], uqv[:, :, c0 + 128:c0 + 160], writes=[wq.r], q="pool", key="ldw")
                    wk = wk_ring.next()
                    P.dma(wk.ap[:, :, 0:128], ukv[:, :, h * 128:(h + 1) * 128], writes=[wk.r], q="pool", key="ldw")
                    P.dma(wk.ap[:, :, 128:256], uvv[:, :, h * 128:(h + 1) * 128], writes=[wk.r], q="pool", key="ldw")
                    for (o, n) in tbs:
                        bk = banks.next()
                        for k in range(4):
                            P.mm(bk.ap[:, 0:n], wq.ap[:, k, 0:128], qdg.ap[:, k, o:o + n], k == 0, k == 3,
                                 [wq.r, qdg.r], [bk.r])
                        P.tt(qn.ap[:, o:o + n], bk.ap[:, 0:n], rq.ap[:, o:o + n], ALU.mult, [bk.r, rq.r], [qn.r])
                        bk = banks.next()
                        for k in range(4):
                            P.mm(bk.ap[0:64, 0:n], wq.ap[:, k, 128:192], qdg.ap[:, k, o:o + n], k == 0, k == 3,
                                 [wq.r, qdg.r], [bk.r])
                        if not samp:
                            P.tt(qr.ap[0:64, o:o + n], bk.ap[0:64, 0:n], rq.ap[0:64, o:o + n], ALU.mult,
                                 [bk.r, rq.r], [qr.r])
                        else:
                            bk2 = banks.next()
                            for k in range(4):
                                P.mm(bk2.ap[0:64, 0:n], wq.ap[:, k, 192:256], qdg.ap[:, k, o:o + n], k == 0, k == 3,
                                     [wq.r, qdg.r], [bk2.r])
                            tm = tmpr.next()
                            P.tt(tm.ap[0:64, 0, 0:n], bk.ap[0:64, 0:n], rope.ap[0:64, 0, o:o + n], ALU.mult,
                                 [bk.r, rope.r], [tm.r])
                            P.tt(tm.ap[0:64, 1, 0:n], bk2.ap[0:64, 0:n], rope.ap[0:64, 1, o:o + n], ALU.mult,
                                 [bk2.r, rope.r], [tm.r])
                            P.tt(tm.ap[0:64, 0, 0:n], tm.ap[0:64, 0, 0:n], tm.ap[0:64, 1, 0:n], ALU.add,
                                 [tm.r], [tm.r])
                            P.tt(qr.ap[0:64, o:o + n], tm.ap[0:64, 0, 0:n], rq.ap[0:64, o:o + n], ALU.mult,
                                 [tm.r, rq.r], [qr.r])
                    for (o, n) in kbs:
                        bk = banks.next()
                        for k in range(2):
                            P.mm(bk.ap[:, 0:n], wk.ap[:, k, 0:128], ckv_all.ap[:, k, o:o + n], k == 0, k == 1,
                                 [wk.r, ckv_all.r], [bk.r])
                        P.copy(P.ev(), kn.ap[:, o:o + n], bk.ap[:, 0:n], [bk.r], [kn.r])
                    for g0 in range(0, nkt, 4):
                        ng = min(4, nkt - g0)
                        bk = banks.next()
                        for i in range(ng):
                            kt = g0 + i
                            for k in range(2):
                                P.mm(bk.ap[:, i * 128:(i + 1) * 128], ckv_all.ap[:, k, kt * 128:(kt + 1) * 128],
                                     wk.ap[:, k, 128:256], k == 0, k == 1, [wk.r, ckv_all.r], [bk.r],
                                     inc=(k == 1 and i == ng - 1))
                        P.copy(P.ev(), vh.ap[:, g0:g0 + ng, :],
                               bk.ap[:, 0:ng * 128].rearrange("p (a b) -> p a b", a=ng), [bk.r], [vh.r])
                    for (o, n, kts) in ablocks:
                        bo = banks.take()
                        bd = banks.take()

                        def scores(kt):
                            bs = banks.next()
                            P.mm(bs.ap[:, 0:n], kn.ap[:, kt * 128:(kt + 1) * 128], qn.ap[:, o:o + n], True, False,
                                 [kn.r, qn.r], [bs.r])
                            P.mm(bs.ap[:, 0:n], kr_all.ap[0:64, kt * 128:(kt + 1) * 128], qr.ap[0:64, o:o + n],
                                 False, True, [kr_all.r, qr.r], [bs.r])
                            return bs
                        bs_next = scores(kts[0])
                        for ki, kt in enumerate(kts):
                            bs = bs_next
                            first, last = (ki == 0), (ki == len(kts) - 1)
                            if not last:
                                bs_next = scores(kts[ki + 1])
                            pt = pring.next()
                            P.act(pt.ap[:, 0:n], bs.ap[:, 0:n], AF.Exp, [bs.r], [pt.r], scale=SCALE)
                            P.mm(bo.ap[:, 0:n], vh.ap[:, kt, :], pt.ap[:, 0:n], first, last, [vh.r, pt.r], [bo.r])
                            P.mm(bd.ap[:, 0:n], ones_b, pt.ap[:, 0:n], first, last, [cb.r, pt.r], [bd.r])
                        rc = rring.next()
                        P.call("dve", "reciprocal", (rc.ap[:, 0:n], bd.ap[:, 0:n]), [bd.r], [rc.r])
                        ot = oring.next()
                        P.tt(ot.ap[:, 0:n], bo.ap[:, 0:n], rc.ap[:, 0:n], ALU.mult, [bo.r, rc.r], [ot.r])
                        P.dma(mixT[h * 128:(h + 1) * 128, t0 + o:t0 + o + n], ot.ap[:, 0:n], reads=[ot.r],
                              q="act", key="st")
                        banks.give(bo)
                        banks.give(bd)
                P.barrier()
                sb.off = mark

        if "D" in stages:
            upv = {0: (w_up_f, a_up_f, "w0f", "a0f"), 1: (w_up_b, a_up_b, "w0b", "a0b")}
            dgroups = [(0, NS, 1, True, 0), (NS, NP_, 2, False, 0)]
            if dbg_seqs is not None:
                dgroups = [dgroups[min(i, 1)] for i in sorted(set(min(i, 1) for i in dbg_seqs))]
            for (t0, Sq, nseq, samp, pi) in dgroups:
                mark = sb.off
                S = Sq * nseq
                NCH = S // C
                NCS = Sq // C
                tbs = [(o, min(512, S - o)) for o in range(0, S, 512)]
                RB = OFF_RW
                pool8 = [sb.t([S]) for _ in range(5)]
                free = list(pool8)
                raw_ring = Ring(sb.ring(1, [nseq * (Sq + 2)]))

                def shift_load(blk_i, out_t):
                    rw_ = raw_ring.next()
                    r0 = RB + blk_i * 128
                    r3 = rw_.ap.rearrange("p (q j) -> p q j", j=Sq + 2)
                    q3 = lambda ap: ap.rearrange("p (q j) -> p q j", j=Sq)
                    P.call("dve", "memset", (r3[:, :, 0:1], 0.0), [], [rw_.r])
                    P.call("dve", "memset", (r3[:, :, Sq + 1:Sq + 2], 0.0), [], [rw_.r])
                    P.dma(r3[:, :, 1:Sq + 1], projT[r0:r0 + 128, t0:t0 + S].rearrange("p (q j) -> p q j", j=Sq),
                          writes=[rw_.r])
                    tmp = free.pop()
                    P.tt(q3(tmp.ap), r3[:, :, 0:Sq], r3[:, :, 2:Sq + 2], ALU.add, [rw_.r], [tmp.r])
                    P.act(q3(out_t.ap), r3[:, :, 1:Sq + 1], AF.Identity, [rw_.r, dv.r], [out_t.r], scale=omm(blk_i))
                    P.stt(out_t.ap, tmp.ap, hmu(blk_i), out_t.ap, ALU.mult, ALU.add, [tmp.r, out_t.r, dv.r], [out_t.r])
                    free.append(tmp)

                twd = sb.t([S], PREC["lora"])
                sgd = sb.t([S], PREC["lora"])
                t_ = free.pop()
                shift_load(24, t_)
                P.act(twd.ap[0:64], t_.ap[0:64], AF.Tanh, [t_.r], [twd.r])
                P.copy("dve", twd.ap[64:128], t_.ap[64:128], [t_.r], [twd.r])
                shift_load(25, t_)
                P.act(sgd.ap, t_.ap, AF.Sigmoid, [t_.r], [sgd.r])
                free.append(t_)
                lw = sb.t([2, 1024], PREC["lora"])
                for d_ in (0, 1):
                    P.dma(lw.ap[0:64, d_, :], upv[d_][0], writes=[lw.r], q="pool", key="ldw")
                    P.dma(lw.ap[64:128, d_, :], upv[d_][1], writes=[lw.r], q="pool", key="ldw")
                gw = sb.t([1024], PREC["lora"])
                P.dma(gw.ap, g_up, writes=[gw.r], q="pool", key="ldw")
                r_t, v_t, kk_t, kds_t = (sb.t([S]) for _ in range(4))
                y_ap = kk_t.ap
                y_r = [Res() for _ in range(NCH)]
                g_t = sb.t([S], BF16)
                vtok = sb.t([NCH, 128], BF16)
                aq = [sb.t([NCH, 2, C], BF16) for _ in range(2)]
                bt = [sb.t([S], BF16) for _ in range(2)]
                kt_ = [sb.t([S], BF16) for _ in range(2)]
                gam = [sb.t([NCH]) for _ in range(2)]
                S32 = [[sb.t([64]) for _ in range(nseq)] for _ in range(2)]
                Sg = [[sb.t([64]) for _ in range(nseq)] for _ in range(2)]
                Sb_ = [[sb.t([64], BF16) for _ in range(nseq)] for _ in range(2)]
                tsw_g = TSW if samp else 7
                if samp:
                    mring = Ring(rtiles[0:8])
                    pr_ring = Ring(rtiles[8:12])
                else:
                    mring = Ring(sb.ring(8, [512]))
                    pr_ring = Ring(sb.ring(4, [512]))
                mbring = Ring(sb.ring(8, [512], BF16))
                pbring = Ring(sb.ring(4, [512], BF16))
                aring = [[Ring(sb.ring(2, [2, 384], BF16)) for _ in range(nseq)] for _ in range(2)]
                xring = [[Ring(sb.ring(2, [128], BF16)) for _ in range(nseq)] for _ in range(2)]
                uring = [[Ring(sb.ring(2, [128], BF16)) for _ in range(nseq)] for _ in range(2)]
                orng = Ring(sb.ring(1, [512], BF16))
                sto = Ring(sb.ring(2, [128]))
                stin = sb.t([2, 64], parts=(0, 64))
                pview = lambda bk_: bk_.ap[:, 0:256].bitcast(BF16)

                for hp in pair_list:
                    k_t = free.pop()
                    shift_load(hp, r_t)
                    shift_load(8 + hp, k_t)
                    shift_load(16 + hp, v_t)
                    sqt = free.pop()
                    P.ts(kk_t.ap, k_t.ap, vcol("k_k", hp), None, ALU.mult, None, [k_t.r, vec.r], [kk_t.r] + y_r)
                    P.act(sqt.ap, kk_t.ap, AF.Square, [kk_t.r], [sqt.r])
                    rs = free.pop()
                    for (o, n) in tbs:
                        bk = banks.next()
                        P.mm(bk.ap[:, 0:n], blk_f, sqt.ap[:, o:o + n], True, True, [cst.r, sqt.r], [bk.r])
                        P.ts(rs.ap[:, o:o + n], bk.ap[:, 0:n], 64.0, 1e-24, ALU.mult, ALU.max, [bk.r], [rs.r])
                    P.act(rs.ap, rs.ap, AF.Sqrt, [rs.r], [rs.r])
                    P.call("dve", "reciprocal", (rs.ap, rs.ap), [rs.r], [rs.r])
                    P.tt(kk_t.ap, kk_t.ap, rs.ap, ALU.mult, [kk_t.r, rs.r], [kk_t.r])
                    free.append(rs)
                    free.append(sqt)
                    for (o, n) in tbs:
                        bk = banks.next()
                        P.mm(bk.ap[:, 0:n], gw.ap[:, hp * 128:(hp + 1) * 128], sgd.ap[:, o:o + n], True, True,
                             [gw.r, sgd.r], [bk.r])
                        P.copy("act", g_t.ap[:, o:o + n], bk.ap[:, 0:n], [bk.r], [g_t.r])
                    vb = free.pop()
                    vbb = vb.ap.bitcast(BF16)[:, 0:S]
                    P.copy("act", vbb, v_t.ap, [v_t.r], [vb.r])
                    for c0 in range(0, NCH, 4):
                        ng = min(4, NCH - c0)
                        bk = banks.next()
                        pb = pview(bk)
                        for i in range(ng):
                            P.tr(pb[:, i * 128:(i + 1) * 128], vbb[:, (c0 + i) * C:(c0 + i + 1) * C], ident_b,
                                 [vb.r, cb.r], [bk.r], inc=(i == ng - 1))
                        P.copy(P.ev(), vtok.ap[:, c0:c0 + ng, :], pb[:, 0:ng * 128].rearrange("p (a b) -> p a b", a=ng),
                               [bk.r], [vtok.r])
                    free.append(vb)

                    for d_ in (0, 1):
                        rev = (d_ == 1)
                        V = (lambda ap: ap[:, ::-1]) if rev else (lambda ap: ap)
                        aq_d, bt_d, kt_d, gam_d = aq[d_], bt[d_], kt_[d_], gam[d_]
                        ld = free.pop()
                        a_ = free.pop()
                        for (o, n) in tbs:
                            bk = banks.next()
                            P.mm(bk.ap[:, 0:n], lw.ap[0:64, d_, hp * 128:(hp + 1) * 128], twd.ap[0:64, o:o + n],
                                 True, True, [lw.r, twd.r], [bk.r])
                            P.act(ld.ap[:, o:o + n], bk.ap[:, 0:n], AF.Sigmoid, [bk.r, vec.r], [ld.r],
                                  bias=vcol(upv[d_][2], hp))
                            bk = banks.next()
                            P.mm(bk.ap[:, 0:n], lw.ap[64:128, d_, hp * 128:(hp + 1) * 128], twd.ap[64:128, o:o + n],
                                 True, True, [lw.r, twd.r], [bk.r])
                            P.act(a_.ap[:, o:o + n], bk.ap[:, 0:n], AF.Sigmoid, [bk.r, vec.r], [a_.r],
                                  bias=vcol(upv[d_][3], hp))
                        kd = free.pop()
                        P.act(kd.ap, a_.ap, AF.Identity, [a_.r, vec.r, dv.r], [kd.r], scale=vcol("k_a", hp),
                              bias=omka(hp))
                        P.tt(kd.ap, kd.ap, k_t.ap, ALU.mult, [kd.r, k_t.r], [kd.r])
                        if d_ == 0:
                            P.copy("act", kds_t.ap, kd.ap, [kd.r], [kds_t.r])
                        else:
                            P.tt(kds_t.ap, kds_t.ap, kd.ap, ALU.add, [kds_t.r, kd.r], [kds_t.r])
                        cl = free.pop()
                        for cc_ in range(NCH):
                            cs = slice(cc_ * C, (cc_ + 1) * C)
                            P.call("dve", "tensor_tensor_scan", (V(cl.ap[:, cs]), ones_f, V(ld.ap[:, cs]), 0.0, ALU.mult, ALU.add),
                                   [ld.r, cst.r], [cl.r])
                        P.tt(ld.ap, cl.ap, ld.ap, ALU.subtract, [cl.r, ld.r], [ld.r])
                        P.act(ld.ap, ld.ap, AF.Exp, [ld.r], [ld.r], scale=NEG_EXP_HALF)
                        P.stt(aq_d.ap[:, :, 0, :], kk_t.ap.rearrange("p (c i) -> p c i", i=C), -1.0,
                              ld.ap.rearrange("p (c i) -> p c i", i=C), ALU.mult, ALU.mult, [kk_t.r, ld.r], [aq_d.r])
                        P.act(ld.ap, cl.ap, AF.Exp, [cl.r], [ld.r], scale=NEG_EXP_HALF)
                        P.tt(aq_d.ap[:, :, 1, :], r_t.ap.rearrange("p (c i) -> p c i", i=C),
                             ld.ap.rearrange("p (c i) -> p c i", i=C), ALU.mult, [r_t.r, ld.r], [aq_d.r])
                        gcol = 0 if rev else C - 1
                        P.copy("dve", gam_d.ap, ld.ap.rearrange("p (c i) -> p c i", i=C)[:, :, gcol], [ld.r], [gam_d.r])
                        P.act(cl.ap, cl.ap, AF.Exp, [cl.r], [cl.r], scale=-NEG_EXP_HALF)
                        P.tt(kt_d.ap, kd.ap, cl.ap, ALU.mult, [kd.r, cl.r], [kt_d.r])
                        P.tt(a_.ap, a_.ap, kk_t.ap, ALU.mult, [a_.r, kk_t.r], [a_.r])
                        P.tt(bt_d.ap, a_.ap, cl.ap, ALU.mult, [a_.r, cl.r], [bt_d.r])
                        for t_x in (cl, kd, a_, ld):
                            free.append(t_x)
                    free.append(k_t)

                    held = [free.pop() for _ in range(4)]
                    btok, ktok, Tt = [], [], []
                    for d_ in (0, 1):
                        p1, p2 = held[2 * d_], held[2 * d_ + 1]
                        h_ = S // 2
                        btok.append(Tl(p1.ap[:, 0:h_].bitcast(BF16).rearrange("p (c f) -> p c f", f=128), p1.r))
                        ktok.append(Tl(p1.ap[:, h_:S].bitcast(BF16).rearrange("p (c f) -> p c f", f=128), p1.r))
                        Tt.append(Tl(p2.ap.bitcast(BF16).rearrange("p (c h f) -> p c h f", h=2, f=128), p2.r))
                    for d_ in (0, 1):
                        rev = (d_ == 1)
                        aq_d, bt_d, kt_d = aq[d_], bt[d_], kt_[d_]
                        for (src_t, dst_t, eng_) in ((bt_d, btok[d_], "act"), (kt_d, ktok[d_], "dve")):
                            for c0 in range(0, NCH, 4):
                                ng = min(4, NCH - c0)
                                bk = banks.next()
                                pb = pview(bk)
                                for i in range(ng):
                                    P.tr(pb[:, i * 128:(i + 1) * 128], src_t.ap[:, (c0 + i) * C:(c0 + i + 1) * C], ident_b,
                                         [src_t.r, cb.r], [bk.r], inc=(i == ng - 1))
                                P.copy(eng_, dst_t.ap[:, c0:c0 + ng, :],
                                       pb[:, 0:ng * 128].rearrange("p (a b) -> p a b", a=ng), [bk.r], [dst_t.r])
                        mN = mask_b["sl"] if rev else mask_b["su"]
                        mNT = mask_b["su"] if rev else mask_b["sl"]
                        Tt_d = Tt[d_]
                        for c0 in range(0, NCH, 4):
                            nu = min(4, NCH - c0)
                            W_ = nu * 128
                            m4 = lambda ap: ap[:, 0:W_].rearrange("p (u f) -> p u f", u=nu)
                            bc4 = lambda ap: ap.unsqueeze(1).to_broadcast([128, nu, 128])
                            stt_ = {}
                            for hh in range(2):
                                p0 = hh * 64
                                bM, bMT = banks.next(), banks.next()
                                for u in range(nu):
                                    cc_ = c0 + u
                                    P.mm(bM.ap[:, u * 128:(u + 1) * 128], bt_d.ap[p0:p0 + 64, cc_ * C:(cc_ + 1) * C],
                                         aq_d.ap[p0:p0 + 64, cc_, 0, :], True, True, [bt_d.r, aq_d.r], [bM.r], inc=(u == nu - 1))
                                for u in range(nu):
                                    cc_ = c0 + u
                                    P.mm(bMT.ap[:, u * 128:(u + 1) * 128], aq_d.ap[p0:p0 + 64, cc_, 0, :],
                                         bt_d.ap[p0:p0 + 64, cc_ * C:(cc_ + 1) * C], True, True, [bt_d.r, aq_d.r], [bMT.r],
                                         inc=(u == nu - 1))
                                M, MT, Pm = mring.next(), mring.next(), pr_ring.next()
                                P.tt(m4(M.ap), m4(bM.ap), bc4(mN), ALU.mult, [bM.r, cb.r], [M.r])
                                P.tt(m4(MT.ap), m4(bMT.ap), bc4(mNT), ALU.mult, [bMT.r, cb.r], [MT.r])
                                P.tt(m4(Pm.ap), m4(M.ap), bc4(ident_b), ALU.add, [M.r, cb.r], [Pm.r])
                                stt_[hh] = [M, MT, Pm, None]
                            for lvl in range(1, 7):
                                last = (lvl == 6)
                                lowp = (lvl >= tsw_g)
                                nlow = (lvl + 1 >= tsw_g)
                                nb = {}
                                for hh in range(2):
                                    M, MT, Pm, Pb = stt_[hh]
                                    bMT2 = banks.next()
                                    for u in range(nu):
                                        sl = slice(u * 128, (u + 1) * 128)
                                        P.mm(bMT2.ap[:, sl], M.ap[:, sl], MT.ap[:, sl], True, True, [M.r, MT.r], [bMT2.r],
                                             inc=(u == nu - 1))
                                    bM2 = None
                                    if not last:
                                        bM2 = banks.next()
                                        for u in range(nu):
                                            sl = slice(u * 128, (u + 1) * 128)
                                            P.mm(bM2.ap[:, sl], MT.ap[:, sl], M.ap[:, sl], True, True, [M.r, MT.r],
                                                 [bM2.r], inc=(u == nu - 1))
                                    nb[hh] = (bMT2, bM2)
                                nm = {}
                                for hh in range(2):
                                    bMT2, bM2 = nb[hh]
                                    MT2 = (mbring if lowp else mring).next()
                                    P.copy("act", MT2.ap[:, 0:W_], bMT2.ap[:, 0:W_], [bMT2.r], [MT2.r])
                                    M2n, MT2n = None, MT2
                                    if not last:
                                        if nlow and not lowp:
                                            MT2n = mbring.next()
                                            P.copy("act", MT2n.ap[:, 0:W_], bMT2.ap[:, 0:W_], [bMT2.r], [MT2n.r])
                                        M2n = (mbring if nlow else mring).next()
                                        P.copy("dve", M2n.ap[:, 0:W_], bM2.ap[:, 0:W_], [bM2.r], [M2n.r])
                                    nm[hh] = (M2n, MT2n, MT2)
                                bPs = {}
                                for hh in range(2):
                                    M2n, MT2n, MT2 = nm[hh]
                                    Pm, Pb = stt_[hh][2], stt_[hh][3]
                                    bP = banks.next()
                                    rhsP = Pb if lowp else Pm
                                    for u in range(nu):
                                        sl = slice(u * 128, (u + 1) * 128)
                                        P.mm(bP.ap[:, sl], MT2.ap[:, sl], rhsP.ap[:, sl], True, True, [MT2.r, rhsP.r], [bP.r],
                                             inc=(u == nu - 1))
                                    bPs[hh] = bP
                                for hh in range(2):
                                    M2n, MT2n, MT2 = nm[hh]
                                    Pm = stt_[hh][2]
                                    bP = bPs[hh]
                                    if last:
                                        P.tt(Tt_d.ap[:, c0:c0 + nu, hh, :], m4(bP.ap), m4(Pm.ap), ALU.add, [bP.r, Pm.r], [Tt_d.r])
                                    else:
                                        Pn = pr_ring.next()
                                        P.tt(Pn.ap[:, 0:W_], bP.ap[:, 0:W_], Pm.ap[:, 0:W_], ALU.add, [bP.r, Pm.r], [Pn.r])
                                        Pbn = None
                                        if nlow:
                                            Pbn = pbring.next()
                                            P.copy("act", Pbn.ap[:, 0:W_], Pn.ap[:, 0:W_], [Pn.r], [Pbn.r])
                                        stt_[hh] = [M2n, MT2n, Pn, Pbn]

                    for d_ in (0, 1):
                        for q_ in range(nseq):
                            if samp:
                                src = (st_b if d_ == 1 else st_f)[2 * hp:2 * hp + 2].rearrange("h v k -> v h k")
                                P.dma(stin.ap, src, writes=[stin.r])
                                bk = banks.next()
                                P.tr(bk.ap[:, 0:64], stin.ap.rearrange("p h k -> p (h k)"), ident[0:64, 0:64],
                                     [stin.r, cst.r], [bk.r])
                                P.copy("dve", S32[d_][q_].ap, bk.ap[:, 0:64], [bk.r], [S32[d_][q_].r])
                            else:
                                P.call("dve", "memset", (S32[d_][q_].ap, 0.0), [], [S32[d_][q_].r])
                            P.copy("act", Sb_[d_][q_].ap, S32[d_][q_].ap, [S32[d_][q_].r], [Sb_[d_][q_].r])
                    P.call("dve", "memset", (y_ap, 0.0), [], [kk_t.r] + y_r)

                    def chunk_step(d_, q_, cc_):
                        rev = (d_ == 1)
                        aq_d, bt_d, kt_d, gam_d = aq[d_], bt[d_], kt_[d_], gam[d_]
                        btok_d, ktok_d, Tt_d = btok[d_], ktok[d_], Tt[d_]
                        S32_d, Sg_d, Sb_d = S32[d_][q_], Sg[d_][q_], Sb_[d_][q_]
                        m_s = mask_b["sl"] if rev else mask_b["su"]
                        m_i = mask_b["il"] if rev else mask_b["iu"]
                        cs = slice(cc_ * C, (cc_ + 1) * C)
                        A = aring[d_][q_].next()
                        for hh in range(2):
                            p0 = hh * 64
                            bA = banks.next()
                            P.mm(bA.ap[:, 0:256], kt_d.ap[p0:p0 + 64, cs],
                                 aq_d.ap[p0:p0 + 64, cc_, :, :].rearrange("p a b -> p (a b)"), True, True,
                                 [kt_d.r, aq_d.r], [bA.r], inc=False)
                            P.mm(bA.ap[:, 256:384], bt_d.ap[p0:p0 + 64, cs], aq_d.ap[p0:p0 + 64, cc_, 1, :], True, True,
                                 [bt_d.r, aq_d.r], [bA.r], inc=True)
                            P.tt(A.ap[:, hh, :], bA.ap[:, 0:384], mask3[d_], ALU.mult, [bA.r, cb.r], [A.r])
                        P.act(Sg_d.ap, S32_d.ap, AF.Copy, [S32_d.r, gam_d.r], [Sg_d.r], scale=gam_d.ap[:, cc_:cc_ + 1])
                        bX = banks.next()
                        for hh in range(2):
                            p0 = hh * 64
                            P.mm(bX.ap[:, hh * 64:(hh + 1) * 64], aq_d.ap[p0:p0 + 64, cc_, 0, :], Sb_d.ap[p0:p0 + 64, :],
                                 True, False, [aq_d.r, Sb_d.r], [bX.r], inc=False)
                            P.mm(bX.ap[:, hh * 64:(hh + 1) * 64], A.ap[:, hh, 0:128], vtok.ap[:, cc_, p0:p0 + 64],
                                 False, True, [A.r, vtok.r], [bX.r], inc=(hh == 1))
                        Xb = xring[d_][q_].next()
                        P.copy("act", Xb.ap, bX.ap[:, 0:128], [bX.r], [Xb.r])
                        bU = banks.next()
                        for hh in range(2):
                            P.mm(bU.ap[:, hh * 64:(hh + 1) * 64], Tt_d.ap[:, cc_, hh, :], Xb.ap[:, hh * 64:(hh + 1) * 64],
                                 True, True, [Tt_d.r, Xb.r], [bU.r], inc=(hh == 1))
                        Ub = uring[d_][q_].next()
                        P.copy("act", Ub.ap, bU.ap[:, 0:128], [bU.r], [Ub.r])
                        bS = banks.next()
                        for hh in range(2):
                            p0 = hh * 64
                            P.mm(bS.ap[p0:p0 + 64, 0:64], btok_d.ap[:, cc_, p0:p0 + 64], Ub.ap[:, p0:p0 + 64],
                                 True, False, [btok_d.r, Ub.r], [bS.r], inc=False)
                            P.mm(bS.ap[p0:p0 + 64, 0:64], ktok_d.ap[:, cc_, p0:p0 + 64], vtok.ap[:, cc_, p0:p0 + 64],
                                 False, True, [ktok_d.r, vtok.r], [bS.r], inc=(hh == 1))
                        bY = banks.next()
                        for hh in range(2):
                            p0 = hh * 64
                            P.mm(bY.ap[p0:p0 + 64, 0:128], Sb_d.ap[p0:p0 + 64, :], aq_d.ap[p0:p0 + 64, cc_, 1, :],
                                 True, False, [Sb_d.r, aq_d.r], [bY.r], inc=False)
                            P.mm(bY.ap[p0:p0 + 64, 0:128], Ub.ap[:, p0:p0 + 64], A.ap[:, hh, 256:384],
                                 False, False, [Ub.r, A.r], [bY.r], inc=False)
                            P.mm(bY.ap[p0:p0 + 64, 0:128], vtok.ap[:, cc_, p0:p0 + 64], A.ap[:, hh, 128:256],
                                 False, True, [vtok.r, A.r], [bY.r], inc=(hh == 1))
                        P.stt(S32_d.ap, bS.ap[:, 0:64], gam_d.ap[:, cc_:cc_ + 1], Sg_d.ap, ALU.mult, ALU.add,
                              [bS.r, gam_d.r, Sg_d.r], [S32_d.r])
                        P.copy("act", Sb_d.ap, S32_d.ap, [S32_d.r], [Sb_d.r])
                        P.tt(y_ap[:, cs], y_ap[:, cs], bY.ap[:, 0:128], ALU.add, [bY.r, y_r[cc_]], [y_r[cc_]])

                    for i in range(NCS):
                        for q_ in range(nseq):
                            chunk_step(0, q_, q_ * NCS + i)
                            chunk_step(1, q_, q_ * NCS + NCS - 1 - i)
                    if not samp:
                        for d_ in (0, 1):
                            for q_ in range(nseq):
                                bk = banks.next()
                                P.tr(bk.ap[0:64, 0:128], S32[d_][q_].ap, ident, [S32[d_][q_].r, cst.r], [bk.r])
                                so = sto.next()
                                P.copy("dve", so.ap[0:64, :], bk.ap[0:64, 0:128], [bk.r], [so.r])
                                dst = (nsb if d_ == 1 else nsf)[pi + q_, 2 * hp:2 * hp + 2].rearrange("h v k -> v h k")
                                P.dma(dst, so.ap[0:64, :].rearrange("p (h k) -> p h k", h=2), reads=[so.r], q="act", key="st")
                    for t_x in held:
                        free.append(t_x)
                    P.stt(kds_t.ap, r_t.ap, vcol("r_k", hp), kds_t.ap, ALU.mult, ALU.mult, [r_t.r, kds_t.r, vec.r],
                          [kds_t.r])
                    e1, e2, e3 = free.pop(), free.pop(), free.pop()
                    for (o, n) in tbs:
                        sl = slice(o, o + n)
                        b1, b2, b3 = banks.next(), banks.next(), banks.next()
                        P.mm(b1.ap[:, 0:n], blk_f, kds_t.ap[:, sl], True, True, [cst.r, kds_t.r], [b1.r])
                        P.mm(b2.ap[:, 0:n], blk_f, y_ap[:, sl], True, True, [cst.r] + y_r, [b2.r])
                        s1 = Tl(e1.ap[:, sl], e1.r)
                        P.act(s1.ap[:, 0:n], y_ap[:, sl], AF.Square, y_r, [s1.r])
                        P.mm(b3.ap[:, 0:n], blk_f, s1.ap[:, 0:n], True, True, [cst.r, s1.r], [b3.r])
                        s2 = Tl(e2.ap[:, sl], e2.r)
                        P.stt(s2.ap[:, 0:n], b1.ap[:, 0:n], 64.0, v_t.ap[:, sl], ALU.mult, ALU.mult, [b1.r, v_t.r], [s2.r])
                        mu_ = Tl(e3.ap[:, sl], e3.r)
                        P.copy("act", mu_.ap[:, 0:n], b2.ap[:, 0:n], [b2.r], [mu_.r])
                        P.tt(s1.ap[:, 0:n], mu_.ap[:, 0:n], mu_.ap[:, 0:n], ALU.mult, [mu_.r], [s1.r])
                        P.tt(s1.ap[:, 0:n], b3.ap[:, 0:n], s1.ap[:, 0:n], ALU.subtract, [b3.r, s1.r], [s1.r])
                        P.act(s1.ap[:, 0:n], s1.ap[:, 0:n], AF.Sqrt, [s1.r, epsc.r], [s1.r], bias=eps_gn)
                        P.call("dve", "reciprocal", (s1.ap[:, 0:n], s1.ap[:, 0:n]), [s1.r], [s1.r])
                        P.tt(mu_.ap[:, 0:n], y_ap[:, sl], mu_.ap[:, 0:n], ALU.subtract, y_r + [mu_.r], [mu_.r])
                        P.tt(mu_.ap[:, 0:n], mu_.ap[:, 0:n], s1.ap[:, 0:n], ALU.mult, [mu_.r, s1.r], [mu_.r])
                        P.affine("act", mu_.ap[:, 0:n], mu_.ap[:, 0:n], vcol("gn_g", hp), vcol("gn_b", hp),
                                 [mu_.r, vec.r], [mu_.r])
                        P.tt(mu_.ap[:, 0:n], mu_.ap[:, 0:n], s2.ap[:, 0:n], ALU.add, [mu_.r, s2.r], [mu_.r])
                        ot = orng.next()
                        P.tt(ot.ap[:, 0:n], mu_.ap[:, 0:n], g_t.ap[:, sl], ALU.mult, [mu_.r, g_t.r], [ot.r])
                        P.dma(mixT[1024 + hp * 128:1024 + (hp + 1) * 128, t0 + o:t0 + o + n], ot.ap[:, 0:n],
                              reads=[ot.r], q="act", key="st")
                    for t_x in (e1, e2, e3):
                        free.append(t_x)
                P.barrier()
                sb.off = mark

        def layer_norm_tile(y, stats_t, gbc, bbc, out_ap, reads, out_res):
            for k in range(4):
                P.call("dve", "bn_stats", (stats_t.ap[:, 6 * k:6 * k + 6], y.ap[:, k * 512:(k + 1) * 512]), [y.r], [stats_t.r])
            P.call("dve", "bn_aggr", (stats_t.ap[:, 24:26], stats_t.ap[:, 0:24]), [stats_t.r], [stats_t.r])
            P.ts(stats_t.ap[:, 26:27], stats_t.ap[:, 25:26], LN_EPS, None, ALU.add, None, [stats_t.r], [stats_t.r])
            P.act(stats_t.ap[:, 26:27], stats_t.ap[:, 26:27], AF.Sqrt, [stats_t.r], [stats_t.r])
            P.call("dve", "reciprocal", (stats_t.ap[:, 26:27], stats_t.ap[:, 26:27]), [stats_t.r], [stats_t.r])
            P.stt(stats_t.ap[:, 27:28], stats_t.ap[:, 24:25], -1.0, stats_t.ap[:, 26:27], ALU.mult, ALU.mult,
                  [stats_t.r], [stats_t.r])
            P.act(y.ap, y.ap, AF.Identity, [y.r, stats_t.r], [y.r], scale=stats_t.ap[:, 26:27], bias=stats_t.ap[:, 27:28])
            P.tt(y.ap, y.ap, gbc.ap, ALU.mult, [y.r, gbc.r], [y.r])
            P.tt(out_ap, y.ap, bbc.ap, ALU.add, [y.r, bbc.r] + reads, [out_res])

        if "E" in stages:
            mark = sb.off
            wo = sb.t([16, D], BF16)
            wov = w_out.rearrange("(k p) c -> p k c", p=128)
            for k4 in range(4):
                P.dma(wo.ap[:, k4 * 4:(k4 + 1) * 4, :], wov[:, k4 * 4:(k4 + 1) * 4, :], writes=[wo.r], q="pool", key="ldw")
            lg = sb.t([D]); lb = sb.t([D])
            P.dma(lg.ap, lnv[0:1, :].partition_broadcast(128), writes=[lg.r])
            P.dma(lb.ap, lnv[1:2, :].partition_broadcast(128), writes=[lb.r])
            g1bc = [sb.t([D]), sb.t([D])]
            dg = Ring(sb.ring(2, [128]))
            for cnd_i in range(2):
                for k in range(16):
                    d_t = dg.next()
                    P.ts(d_t.ap, ident, mod.ap[:, 32 + k, cnd_i:cnd_i + 1], None, ALU.mult, None, [cst.r, mod.r], [d_t.r])
                    bk = banks.next()
                    P.mm(bk.ap[:, 0:128], ones_f, d_t.ap, True, True, [cst.r, d_t.r], [bk.r])
                    P.copy(P.ev(), g1bc[cnd_i].ap[:, k * 128:(k + 1) * 128], bk.ap[:, 0:128], [bk.r], [g1bc[cnd_i].r])
            mxr = Ring(sb.ring(2, [16, 128], BF16))
            xr = Ring(sb.ring(2, [D]))
            yr = Ring(sb.ring(2, [D]))
            x1r = Ring(sb.ring(2, [D]))
            hr = Ring(sb.ring(2, [16, 128], BF16))
            stt_ = Ring(sb.ring(2, [32]))
            def e_mm(tt):
                ts_ = slice(tt * 128, (tt + 1) * 128)
                cnd_i = cond_of_tile(tt)
                mx = mxr.next()
                P.dma(mx.ap, mixT[:, ts_].rearrange("(k p) t -> p k t", p=128), writes=[mx.r])
                xt = xr.next()
                P.dma(xt.ap, x_all[ts_, :], writes=[xt.r])
                y = yr.next()
                bks = []
                for cbk in range(4):
                    bk = banks.next()
                    for k in range(16):
                        P.mm(bk.ap, mx.ap[:, k, :], wo.ap[:, k, cbk * 512:(cbk + 1) * 512], k == 0, k == 15,
                             [mx.r, wo.r], [bk.r])
                    bks.append(bk)

                def evac():
                    for cbk in range(4):
                        bk = bks[cbk]
                        P.tt(y.ap[:, cbk * 512:(cbk + 1) * 512], bk.ap, g1bc[cnd_i].ap[:, cbk * 512:(cbk + 1) * 512],
                             ALU.mult, [bk.r, g1bc[cnd_i].r], [y.r])
                    P.stt(y.ap, xt.ap, ALPHA, y.ap, ALU.mult, ALU.add, [xt.r, y.r], [y.r])
                return y, evac

            def e_ln(tt, y):
                ts_ = slice(tt * 128, (tt + 1) * 128)
                cnd_i = cond_of_tile(tt)
                x1 = x1r.next()
                s_ = stt_.next()
                layer_norm_tile(y, s_, lg, lb, x1.ap, [], x1.r)
                P.dma(x1s[ts_, :], x1.ap, reads=[x1.r], q="act", key="st")
                ht = hr.next()
                for q4 in range(4):
                    bk = banks.next()
                    for i in range(4):
                        k = q4 * 4 + i
                        P.tr(bk.ap[:, i * 128:(i + 1) * 128], x1.ap[:, k * 128:(k + 1) * 128], ident,
                             [x1.r, cst.r], [bk.r], inc=(i == 3))
                    eng_ = P.ev()
                    for i in range(4):
                        k = q4 * 4 + i
                        P.affine(eng_, ht.ap[:, k, :], bk.ap[:, i * 128:(i + 1) * 128],
                                 mod1p.ap[:, 16 + k, cnd_i:cnd_i + 1], mod.ap[:, 48 + k, cnd_i:cnd_i + 1],
                                 [bk.r, mod.r, mod1p.r], [ht.r])
                P.dma(hT[:, ts_].rearrange("(k p) t -> p k t", p=128), ht.ap, reads=[ht.r], q="act", key="st")

            y_next, ev_next = e_mm(0)
            ev_next()
            for tt in range(G // 128):
                y_cur = y_next
                if tt + 1 < G // 128:
                    y_next, ev_next = e_mm(tt + 1)
                e_ln(tt, y_cur)
                if tt + 1 < G // 128:
                    ev_next()
            P.barrier()
            sb.off = mark

        if "F" in stages:
            mark = sb.off
            lg = sb.t([D]); lb = sb.t([D])
            P.dma(lg.ap, lnv[2:3, :].partition_broadcast(128), writes=[lg.r])
            P.dma(lb.ap, lnv[3:4, :].partition_broadcast(128), writes=[lb.r])
            actT = sb.t([44, 512], BF16)
            f2 = sb.t([16, 512])
            hb_alias = Tl(f2.ap.rearrange("p k t -> p (k t)")[:, 0:4096].bitcast(BF16).rearrange("p (k t) -> p k t", k=16), f2.r)
            hring = Ring([hb_alias])
            wg_r = Ring(sb.ring(2, [16, 256], BF16))
            wu_r = Ring(sb.ring(2, [16, 256], BF16))
            wd_r = Ring(sb.ring(2, [44, 128], BF16))
            sgr = Ring(sb.ring(2, [512]))
            x1r = Ring(sb.ring(1, [D]))
            yr = Ring(sb.ring(1, [D]))
            stt_ = Ring(sb.ring(2, [32]))
            wgv = w_gate.rearrange("(k p) c -> p k c", p=128)
            wuv = w_upf.rearrange("(k p) c -> p k c", p=128)
            wdv = w_down.rearrange("(k p) c -> p k c", p=128)
            for tb in range(G // 512):
                cnd_i = 0 if tb < NS // 512 else 1
                tsl = slice(tb * 512, (tb + 1) * 512)
                hb = hring.next()
                P.dma(hb.ap, hT[:, tsl].rearrange("(k p) t -> p k t", p=128), writes=[hb.r])
                for fb in range(22):
                    wg, wu = wg_r.next(), wu_r.next()
                    if tb == 0:
                        P.dma(wg.ap, wgv[:, :, fb * 256:(fb + 1) * 256], writes=[wg.r], q="pool", key="ldw")
                        P.dma(wu.ap, wuv[:, :, fb * 256:(fb + 1) * 256], writes=[wu.r], q="pool", key="ldw")
                        P.dma(wg16[fb], wg.ap.rearrange("p k c -> p (k c)"), reads=[wg.r], q="sp", key="st")
                        P.dma(wu16[fb], wu.ap.rearrange("p k c -> p (k c)"), reads=[wu.r], q="sp", key="st")
                    else:
                        P.dma(wg.ap.rearrange("p k c -> p (k c)"), wg16[fb], writes=[wg.r], q="pool", key="ldw")
                        P.dma(wu.ap.rearrange("p k c -> p (k c)"), wu16[fb], writes=[wu.r], q="pool", key="ldw")
                    for sub in range(2):
                        f = fb * 2 + sub
                        bg, bu = banks.next(), banks.next()
                        for k in range(16):
                            P.mm(bg.ap, wg.ap[:, k, sub * 128:(sub + 1) * 128], hb.ap[:, k, :], k == 0, k == 15,
                                 [wg.r, hb.r], [bg.r])
                        for k in range(16):
                            P.mm(bu.ap, wu.ap[:, k, sub * 128:(sub + 1) * 128], hb.ap[:, k, :], k == 0, k == 15,
                                 [wu.r, hb.r], [bu.r])
                        sg = sgr.next()
                        P.act(sg.ap, bg.ap, AF.Silu, [bg.r], [sg.r])
                        P.tt(actT.ap[:, f, :], sg.ap, bu.ap, ALU.mult, [sg.r, bu.r], [actT.r])
                for cbk in range(16):
                    wd = wd_r.next()
                    if tb == 0:
                        P.dma(wd.ap, wdv[:, :, cbk * 128:(cbk + 1) * 128], writes=[wd.r], q="pool", key="ldw")
                        P.dma(wd16[cbk], wd.ap.rearrange("p k c -> p (k c)"), reads=[wd.r], q="sp", key="st")
                    else:
                        P.dma(wd.ap.rearrange("p k c -> p (k c)"), wd16[cbk], writes=[wd.r], q="pool", key="ldw")
                    bk = banks.next()
                    for f in range(44):
                        P.mm(bk.ap, wd.ap[:, f, :], actT.ap[:, f, :], f == 0, f == 43, [wd.r, actT.r], [bk.r])
                    P.affine(P.ev(), f2.ap[:, cbk, :], bk.ap, mod.ap[:, 80 + cbk, cnd_i:cnd_i + 1], 0.0,
                             [bk.r, mod.r], [f2.r])
                for t4 in range(4):
                    tt = tb * 4 + t4
                    ts_ = slice(tt * 128, (tt + 1) * 128)
                    x1 = x1r.next()
                    P.dma(x1.ap, x1s[ts_, :], writes=[x1.r])
                    y = yr.next()
                    for q4 in range(4):
                        bk = banks.next()
                        for i in range(4):
                            k = q4 * 4 + i
                            P.tr(bk.ap[:, i * 128:(i + 1) * 128], f2.ap[:, k, t4 * 128:(t4 + 1) * 128], ident,
                                 [f2.r, cst.r], [bk.r], inc=(i == 3))
                        P.stt(y.ap[:, q4 * 512:(q4 + 1) * 512], x1.ap[:, q4 * 512:(q4 + 1) * 512], ALPHA, bk.ap,
                              ALU.mult, ALU.add, [x1.r, bk.r], [y.r])
                    s_ = stt_.next()
                    layer_norm_tile(y, s_, lg, lb, y.ap, [], y.r)
                    P.dma(y_all[ts_, :], y.ap, reads=[y.r], q="act", key="st")
                if tb == 0:
                    P.barrier()
            P.barrier()
            sb.off = mark

        P.barrier(streams=["sp"])
        P.emit()
    return nc


VEC_SPECS = [("mu", 26), ("qg", 4), ("kvg", 2), ("w0f", 8), ("w0b", 8), ("a0f", 8), ("a0b", 8),
             ("k_k", 8), ("k_a", 8), ("r_k", 8), ("gn_g", 8), ("gn_b", 8)]
VO = {}
_o = 0
for _n, _c in VEC_SPECS:
    VO[_n] = _o
    _o += _c
NVEC = (_o + 7) // 8 * 8
CONST_SPECS = [("ident", 128), ("ones", 128), ("blk", 128), ("m_su", 128), ("m_iu", 128), ("m_sl", 128), ("m_il", 128)]
CO = {}
_o = 0
for _n, _c in CONST_SPECS:
    CO[_n] = _o
    _o += _c
NCONST = _o
NCB = 256 + 4 * 128 + 2 * 384


def _colmajor(v):
    v = np.asarray(v, np.float32).reshape(-1)
    return np.ascontiguousarray(v.reshape(-1, 128).T)


def make_consts():
    c = np.zeros((128, NCONST), np.float32)
    i = np.arange(128)
    c[:, CO["ident"]:CO["ident"] + 128] = np.eye(128, dtype=np.float32)
    c[:, CO["ones"]:CO["ones"] + 128] = 1.0
    c[:, CO["blk"]:CO["blk"] + 128] = ((i[:, None] // 64) == (i[None, :] // 64)).astype(np.float32) / 64.0
    s, t = i[:, None], i[None, :]
    c[:, CO["m_su"]:CO["m_su"] + 128] = (s < t)
    c[:, CO["m_iu"]:CO["m_iu"] + 128] = (s <= t)
    c[:, CO["m_sl"]:CO["m_sl"] + 128] = (s > t)
    c[:, CO["m_il"]:CO["m_il"] + 128] = (s >= t)
    return c


def make_rope():
    n = NS
    rows = n // 64
    row = np.repeat(np.arange(rows), 64).astype(np.float32)
    col = np.tile(np.arange(64), rows).astype(np.float32)
    freqs = (np.float32(10000.0) ** (-np.arange(16, dtype=np.float32) / np.float32(16))).astype(np.float32)
    ang = np.concatenate([row[:, None] * freqs, col[:, None] * freqs], -1).astype(np.float32)
    cos, sin = np.cos(ang).astype(np.float32), np.sin(ang).astype(np.float32)
    t = np.zeros((64, 2, n), np.float32)
    t[0:32, 0] = cos.T
    t[32:64, 0] = cos.T
    t[0:32, 1] = -sin.T
    t[32:64, 1] = sin.T
    return t


def make_core_inputs(inp, core):
    f = lambda a: np.ascontiguousarray(np.asarray(a, np.float32))
    b = core
    x_all = np.concatenate([inp["x_sample"][b], inp["x_prompt"][2 * b], inp["x_prompt"][2 * b + 1]], 0)
    cond = np.stack([inp["c"][b], inp["c_ctx"]], 0)
    condT = np.ascontiguousarray(cond.reshape(2, 16, 128).transpose(2, 1, 0))
    vec = np.zeros((128, NVEC), np.float32)
    src = {"mu": inp["tok_shift_mu"][0], "qg": inp["q_norm_g"][0], "kvg": inp["kv_norm_g"][0],
           "w0f": inp["w0_fwd"][0], "w0b": inp["w0_bwd"][0], "a0f": inp["a0_fwd"][0], "a0b": inp["a0_bwd"][0],
           "k_k": inp["k_k"][0], "k_a": inp["k_a"][0], "r_k": inp["r_k"][0], "gn_g": inp["gn_g"][0],
           "gn_b": inp["gn_b"][0]}
    for n_, c_ in VEC_SPECS:
        vec[:, VO[n_]:VO[n_] + c_] = _colmajor(src[n_])
    return {
        "x_all": f(x_all), "cache_ckv": f(inp["cache_ckv"][b, 0]), "cache_kr": f(inp["cache_krope"][b, 0]),
        "st_f": f(inp["state_wkv_fwd"][b, 0]), "st_b": f(inp["state_wkv_bwd"][b, 0]),
        "condT": f(condT), "bmodT": _colmajor(inp["b_mod"][0]), "vecs": vec,
        "w_mod": f(inp["w_mod"][0]), "w_in": f(inp["w_in"][0]), "w_uq": f(inp["w_uq"][0]),
        "w_uk": f(inp["w_uk"][0]), "w_uv": f(inp["w_uv"][0]),
        "w_up_f": f(inp["w_up_fwd"][0]), "w_up_b": f(inp["w_up_bwd"][0]),
        "a_up_f": f(inp["a_up_fwd"][0]), "a_up_b": f(inp["a_up_bwd"][0]), "g_up": f(inp["g_up"][0]),
        "w_out": f(inp["w_out"][0]),
        "lnv": f(np.stack([inp["ln1_g"][0], inp["ln1_b"][0], inp["ln2_g"][0], inp["ln2_b"][0]], 0)),
        "w_gate": f(inp["w_ffn_gate"][0]), "w_upf": f(inp["w_ffn_up"][0]), "w_down": f(inp["w_ffn_down"][0]),
    }


def kernel(**inputs):
    inp = {k: np.asarray(v) for k, v in inputs.items()}
    nc = build_program()
    consts = make_consts()
    rope = make_rope()
    in_maps = []
    for core in range(8):
        m = make_core_inputs(inp, core)
        m["consts"] = consts
        m["ropeT"] = rope
        in_maps.append(m)
    res = run_bass_kernel_spmd(nc, in_maps, core_ids=list(range(8)))
    R = res.results
    y_prompt = np.zeros((16, 256, D), np.float32)
    y_sample = np.zeros((8, NS, D), np.float32)
    new_ckv = np.zeros((16, 1, 256, 256), np.float32)
    new_kr = np.zeros((16, 1, 256, 64), np.float32)
    new_sf = np.zeros((16, 1, 16, 64, 64), np.float32)
    new_sb = np.zeros((16, 1, 16, 64, 64), np.float32)
    for b in range(8):
        r = R[b]
        y_sample[b] = r["y_all"][0:NS]
        y_prompt[2 * b] = r["y_all"][NS:NS + 256]
        y_prompt[2 * b + 1] = r["y_all"][NS + 256:NS + 512]
        new_ckv[2 * b, 0] = r["nckv"][0:256]
        new_ckv[2 * b + 1, 0] = r["nckv"][256:512]
        new_kr[2 * b, 0] = r["nkr"][0:256]
        new_kr[2 * b + 1, 0] = r["nkr"][256:512]
        new_sf[2 * b, 0] = r["nsf"][0]
        new_sf[2 * b + 1, 0] = r["nsf"][1]
        new_sb[2 * b, 0] = r["nsb"][0]
        new_sb[2 * b + 1, 0] = r["nsb"][1]
    return (y_prompt, y_sample, new_ckv, new_kr, new_sf, new_sb)
```
